# Optimizing a Trainium2 kernel written in Bass

```python
import jax, jax.numpy as jnp
from jax import lax
import numpy as np

D_MODEL = 2048
BATCH = 2
SEQ = 4096
DEPTH = 1

CHUNK = 64
EPS = 1e-6
A_WIDTH = D_MODEL
A_GROUPS = 8
A_GROUP_W = A_WIDTH // A_GROUPS
A_BLOCK = 128
B_HEAD_DIM = 128
B_HEADS = D_MODEL // B_HEAD_DIM
KV_LATENT = D_MODEL // 4
IDX_HEADS = D_MODEL // 128
IDX_DIM = 64
TOPK_MAX = 256
Q_BLOCK = 128
D_FF = ((8 * D_MODEL + 3 * 256 - 1) // (3 * 256)) * 256
SPLIT_SIZES = (2 * A_WIDTH, B_HEADS * B_HEAD_DIM, KV_LATENT, IDX_HEADS * IDX_DIM, IDX_DIM, IDX_HEADS, 2 * D_MODEL)
N_IN = 2 * A_WIDTH + B_HEADS * B_HEAD_DIM + KV_LATENT + IDX_HEADS * IDX_DIM + IDX_DIM + IDX_HEADS + 2 * D_MODEL

kernel_name = "hybrid_gmlp_dsa_block"


def rms_norm(x, g):
    xf = x.astype(jnp.float32)
    y = xf * lax.rsqrt(jnp.mean(xf * xf, axis=-1, keepdims=True) + EPS)
    return (y * g.astype(jnp.float32)).astype(x.dtype)


def layer_norm(x, g, b):
    xf = x.astype(jnp.float32)
    xc = xf - jnp.mean(xf, axis=-1, keepdims=True)
    y = xc * lax.rsqrt(jnp.mean(xc * xc, axis=-1, keepdims=True) + EPS)
    return (y * g.astype(jnp.float32) + b.astype(jnp.float32)).astype(x.dtype)


def chunk_causal(q_pos, k_pos):
    return (k_pos[None, :] // CHUNK) <= (q_pos[:, None] // CHUNK)


def spatial_gating_unit(z, ln_g, ln_b, w_s, b_s):
    bsz, seq, _ = z.shape
    u, v = jnp.split(z, 2, axis=-1)
    v = layer_norm(v, ln_g, ln_b).reshape(bsz, seq // A_BLOCK, A_BLOCK, A_GROUPS, A_GROUP_W)
    pos = jnp.arange(A_BLOCK)
    w = jnp.where(chunk_causal(pos, pos)[None], w_s, jnp.zeros((), w_s.dtype))
    sv = jnp.einsum("gpq,bnqgc->bnpgc", w, v) + b_s.T[None, None, :, :, None]
    return u * sv.reshape(bsz, seq, A_WIDTH)


def indexer_sparse_attention(q, c_kv, q_idx, k_idx, w_idx, w_uk, w_uv):
    bsz, seq = q.shape[0], q.shape[1]
    n_keys = c_kv.shape[1]
    k_sel = min(TOPK_MAX, n_keys // 4)
    n_blk = seq // Q_BLOCK
    key_pos = jnp.arange(n_keys)

    def to_blocks(a):
        return a.reshape((bsz, n_blk, Q_BLOCK) + a.shape[2:]).swapaxes(0, 1)

    def one_block(args):
        qb, qib, wib, start = args
        q_pos = start + jnp.arange(Q_BLOCK)
        adm = chunk_causal(q_pos, key_pos)
        logits = jnp.einsum("bthd,bsd->bths", qib, k_idx).astype(jnp.float32) * IDX_DIM ** -0.5
        score = jnp.einsum("bth,bths->bts", wib.astype(jnp.float32) * IDX_HEADS ** -0.5, jax.nn.relu(logits))
        score = jnp.where(adm[None], score, -jnp.inf)
        _, sel = lax.top_k(score, k_sel)
        valid = (sel // CHUNK) <= (q_pos[None, :, None] // CHUNK)
        c_sel = jax.vmap(lambda c, i: c[i])(c_kv, sel)
        q_lat = jnp.einsum("bthd,chd->bthc", qb, w_uk)
        att = jnp.einsum("bthc,btkc->bthk", q_lat, c_sel).astype(jnp.float32) * B_HEAD_DIM ** -0.5
        att = jnp.where(valid[:, :, None, :], att, -jnp.inf)
        p = jax.nn.softmax(att, axis=-1).astype(c_sel.dtype)
        o_lat = jnp.einsum("bthk,btkc->bthc", p, c_sel)
        return jnp.einsum("bthc,chd->bthd", o_lat, w_uv)

    starts = jnp.arange(n_blk, dtype=jnp.int32) * Q_BLOCK
    out = lax.map(one_block, (to_blocks(q), to_blocks(q_idx), to_blocks(w_idx), starts))
    return out.swapaxes(0, 1).reshape(bsz, seq, B_HEADS * B_HEAD_DIM)


def setup_inputs(seed: int = 0) -> dict:
    key = jax.random.key(seed)
    ks = jax.random.split(key, 18)
    f32 = jnp.float32

    def nrm(k, shape, scale):
        return jax.random.normal(k, shape, f32) * scale

    def gain(k, shape):
        return 1.0 + 0.02 * jax.random.normal(k, shape, f32)

    L = DEPTH
    return {
        "x": nrm(ks[0], (BATCH, SEQ, D_MODEL), 1.0),
        "norm1_g": gain(ks[1], (L, D_MODEL)),
        "w_in": nrm(ks[2], (L, D_MODEL, N_IN), D_MODEL ** -0.5),
        "a_ln_g": gain(ks[3], (L, A_WIDTH)),
        "a_ln_b": nrm(ks[4], (L, A_WIDTH), 0.02),
        "a_w_s": nrm(ks[5], (L, A_GROUPS, A_BLOCK, A_BLOCK), A_BLOCK ** -0.5),
        "a_b_s": gain(ks[6], (L, A_GROUPS, A_BLOCK)),
        "kv_norm_g": gain(ks[7], (L, KV_LATENT)),
        "w_uk": nrm(ks[8], (L, KV_LATENT, B_HEADS, B_HEAD_DIM), KV_LATENT ** -0.5),
        "w_uv": nrm(ks[9], (L, KV_LATENT, B_HEADS, B_HEAD_DIM), KV_LATENT ** -0.5),
        "w_oa": nrm(ks[10], (L, A_WIDTH, D_MODEL), A_WIDTH ** -0.5),
        "w_ob": nrm(ks[11], (L, B_HEADS * B_HEAD_DIM, D_MODEL), (B_HEADS * B_HEAD_DIM) ** -0.5),
        "w_out": nrm(ks[12], (L, D_MODEL, D_MODEL), D_MODEL ** -0.5),
        "norm2_g": gain(ks[13], (L, D_MODEL)),
        "w_ff_gate": nrm(ks[14], (L, D_MODEL, D_FF), D_MODEL ** -0.5),
        "w_ff_up": nrm(ks[15], (L, D_MODEL, D_FF), D_MODEL ** -0.5),
        "w_ff_down": nrm(ks[16], (L, D_FF, D_MODEL), D_FF ** -0.5),
        "final_g": gain(ks[17], (D_MODEL,)),
    }


def reference(x, norm1_g, w_in, a_ln_g, a_ln_b, a_w_s, a_b_s, kv_norm_g, w_uk, w_uv, w_oa, w_ob, w_out, norm2_g, w_ff_gate, w_ff_up, w_ff_down, final_g):
    bsz, seq, _ = x.shape
    cuts = []
    offset = 0
    for n in SPLIT_SIZES[:-1]:
        offset += n
        cuts.append(offset)
    h = x
    for l in range(DEPTH):
        xn = rms_norm(h, norm1_g[l])
        proj = xn @ w_in[l]
        z_a, q, c_kv, q_idx, k_idx, w_idx, gates = jnp.split(proj, cuts, axis=-1)
        y_a = spatial_gating_unit(jax.nn.gelu(z_a), a_ln_g[l], a_ln_b[l], a_w_s[l], a_b_s[l])
        y_b = indexer_sparse_attention(
            q.reshape(bsz, seq, B_HEADS, B_HEAD_DIM),
            rms_norm(c_kv, kv_norm_g[l]),
            q_idx.reshape(bsz, seq, IDX_HEADS, IDX_DIM),
            k_idx, w_idx, w_uk[l], w_uv[l])
        g_a, g_b = jnp.split(jax.nn.sigmoid(gates), 2, axis=-1)
        mixed = g_a * (y_a @ w_oa[l]) + g_b * (y_b @ w_ob[l])
        h = h + mixed @ w_out[l]
        hn = rms_norm(h, norm2_g[l])
        h = h + (jax.nn.silu(hn @ w_ff_gate[l]) * (hn @ w_ff_up[l])) @ w_ff_down[l]
    return rms_norm(h, final_g)
```

```python
import numpy as np
from contextlib import ExitStack
import concourse.bass as bass
import concourse.mybir as mybir
from concourse.bass_utils import run_bass_kernel_spmd

F32 = mybir.dt.float32
BF16 = mybir.dt.bfloat16
AF = mybir.ActivationFunctionType
ALU = mybir.AluOpType
AX = mybir.AxisListType

SAME_ENGINE_SYNC = True
NIT = 20
EPS = 1e-6
KB = 1024
ARENA = 204 * KB

C_U, C_V, C_Q, C_CKV, C_QI, C_KI, C_WI, C_GA, C_GB = 0, 2048, 4096, 6144, 6656, 7680, 7744, 7760, 9808
N_IN = 11856
DFF = 5632


class Prog:
    ENG = ("pe", "act", "dve", "pool", "sp")

    def __init__(self, nc):
        self.nc = nc
        self.ops = []
        self.bufs = {}
        self.res = {}

    def eng(self, name):
        nc = self.nc
        return {"pe": nc.tensor, "act": nc.scalar, "dve": nc.vector,
                "pool": nc.gpsimd, "sp": nc.sync}[name]

    def buf(self, name, lo=None, hi=None):
        inherit = {}
        if lo is not None:
            for n, b in self.bufs.items():
                if b["lo"] is not None and b["lo"] < hi and lo < b["hi"]:
                    b["open"] = False
                    for src in (b["touched"], b["inherit"]):
                        for ch, o in src.items():
                            if inherit.get(ch, -1) < o:
                                inherit[ch] = o
        assert name not in self.bufs or not self.bufs[name]["open"], name
        if name in self.bufs:
            self.bufs[name + "#old%d" % len(self.bufs)] = self.bufs[name]
        self.bufs[name] = dict(lo=lo, hi=hi, open=True, touched={}, inherit=inherit)
        for k in [k for k in self.res if k[0] == name]:
            del self.res[k]

    def _chan(self, o):
        return ("e", o["eng"]) if o["kind"] == "c" else ("d", o["sem_key"])

    def _add(self, kind, eng, fn, reads, writes, sem_key=None):
        idx = len(self.ops)
        o = dict(kind=kind, eng=eng, fn=fn, sem_key=sem_key, signal=False)
        ch = self._chan(o)
        deps = {}

        def add(ch2, op2):
            if op2 is None:
                return
            if deps.get(ch2, -1) < op2:
                deps[ch2] = op2
        for key in list(reads) + list(writes):
            b = self.bufs[key[0]]
            assert b["open"], ("closed buffer", key)
            for ch2, op2 in b["inherit"].items():
                add(ch2, op2)
        for key in reads:
            st = self.res.get(key)
            if st is not None and st["w"] is not None:
                add(self._chan(self.ops[st["w"]]), st["w"])
        for key in writes:
            st = self.res.get(key)
            if st is not None:
                if st["w"] is not None:
                    add(self._chan(self.ops[st["w"]]), st["w"])
                for ch2, op2 in st["r"].items():
                    add(ch2, op2)
        for key in reads:
            st = self.res.setdefault(key, dict(w=None, r={}))
            st["r"][ch] = idx
            self.bufs[key[0]]["touched"][ch] = idx
        for key in writes:
            st = self.res.setdefault(key, dict(w=None, r={}))
            st["w"] = idx
            st["r"] = {}
            self.bufs[key[0]]["touched"][ch] = idx
        o["deps"] = deps
        self.ops.append(o)

    def op(self, eng, fn, reads=(), writes=()):
        self._add("c", eng, fn, reads, writes)

    def dma(self, queue, fn, reads=(), writes=(), sem_key=None):
        self._add("d", queue, fn, reads, writes, sem_key=sem_key if sem_key is not None else tuple(writes)[0])

    def _skip(self, o, D):
        return (D["kind"] == "c" and o["kind"] == "c" and D["eng"] == o["eng"]
                and (o["eng"] == "pe" or not SAME_ENGINE_SYNC))

    def emit(self, stack):
        nc = self.nc
        ops = self.ops
        for o in ops:
            for ch, d in o["deps"].items():
                D = ops[d]
                if D["kind"] == "c" and not self._skip(o, D):
                    D["signal"] = True
        esem = {e: stack.enter_context(nc.semaphore("s_" + e)) for e in self.ENG}
        dsem = {}
        for o in ops:
            if o["kind"] == "d" and o["sem_key"] not in dsem:
                dsem[o["sem_key"]] = stack.enter_context(nc.semaphore("d%d" % len(dsem)))
        ecount = {e: 0 for e in self.ENG}
        dcount = {k: 0 for k in dsem}
        for o in ops:
            if o["kind"] == "c":
                if o["signal"]:
                    ecount[o["eng"]] += 1
                o["cnt"] = ecount[o["eng"]]
            else:
                dcount[o["sem_key"]] += 16
                o["cnt"] = dcount[o["sem_key"]]
        waited = {e: {} for e in self.ENG}
        nw = 0
        for o in ops:
            e = o["eng"]
            E = self.eng(e)
            for ch, d in o["deps"].items():
                D = ops[d]
                if self._skip(o, D):
                    continue
                sem = esem[D["eng"]] if D["kind"] == "c" else dsem[D["sem_key"]]
                val = D["cnt"]
                if waited[e].get(ch, -1) >= val:
                    continue
                E.wait_ge(sem, val)
                waited[e][ch] = val
                nw += 1
            inst = o["fn"](E)
            if o["kind"] == "c":
                if o["signal"]:
                    inst.then_inc(esem[e], 1)
            else:
                inst.then_inc(dsem[o["sem_key"]], 16)
        sp = nc.sync
        for k, s in dsem.items():
            if dcount[k] > 0:
                sp.wait_ge(s, dcount[k])
        for e in ("pe", "act", "dve"):
            if ecount[e] > 0:
                sp.wait_ge(esem[e], ecount[e])
        self.nwaits = nw
        self.nsems = len(dsem) + len(esem)


class Rot:
    def __init__(self, items):
        self.items = list(items)
        self.i = 0

    def next(self):
        v = self.items[self.i % len(self.items)]
        self.i += 1
        return v


def build_program(debug=False):
    nc = bass.Bass("TRN2", target_bir_lowering=False)

    def din(name, shape):
        return nc.dram_tensor(name, shape, F32, kind="ExternalInput").ap()

    x_all = din("x_all", [4096, 2048])
    x_own = din("x_own", [1024, 2048])
    admb_d = din("admb", [128, 512])
    ident_d = din("ident", [128, 128])
    pow2_d = din("pow2", [128, 32])
    g1bc_d = din("g1bc", [128, 2048])
    kvgbc_d = din("kvgbc", [128, 512])
    lngbc_d = din("lngbc", [128, 2048])
    lnbbc_d = din("lnbbc", [128, 2048])
    bsbc_d = din("bsbc", [128, 2048])
    wsT_d = din("wsT", [128, 1024])
    g2bc_d = din("g2bc", [128, 2048])
    gfbc_d = din("gfbc", [128, 2048])
    w_in = din("w_in", [2048, N_IN])
    w_uk = din("w_uk", [512, 2048])
    w_uv = din("w_uv", [512, 2048])
    w_oa = din("w_oa", [2048, 2048])
    w_ob = din("w_ob", [2048, 2048])
    w_out = din("w_out", [2048, 2048])
    w_fg = din("w_ff_gate", [2048, DFF])
    w_fu = din("w_ff_up", [2048, DFF])
    w_fd = din("w_ff_down", [DFF, 2048])
    out_d = nc.dram_tensor("out", [1024, 2048], F32, kind="ExternalOutput").ap()

    win = w_in.rearrange("(kc p) n -> p kc n", p=128)
    wuk = w_uk.rearrange("(kc p) n -> p kc n", p=128)
    wuv = w_uv.rearrange("(kc p) n -> p kc n", p=128)
    woa = w_oa.rearrange("(kc p) n -> p kc n", p=128)
    wob = w_ob.rearrange("(kc p) n -> p kc n", p=128)
    wout = w_out.rearrange("(kc p) n -> p kc n", p=128)
    wfg = w_fg.rearrange("(kc p) n -> p kc n", p=128)
    wfu = w_fu.rearrange("(kc p) n -> p kc n", p=128)
    wfd = w_fd.rearrange("(kc p) n -> p kc n", p=128)

    st = ExitStack()
    arena = st.enter_context(nc.sbuf_tensor("arena", [128, ARENA // 4], F32))
    identb = st.enter_context(nc.sbuf_tensor("identb", [128, 128], BF16))
    ps = [st.enter_context(nc.psum_tensor("ps%d" % i, [128, 512], F32)) for i in range(8)]
    psb = [p[:].bitcast(BF16) for p in ps]
    P = Prog(nc)

    def V(off, shape, dt):
        sz = 4 if dt == F32 else 2
        n = int(np.prod(shape)) * sz
        assert off % 4 == 0 and n % 4 == 0 and off + n <= ARENA, (off, shape)
        ap = arena[:, off // 4:(off + n) // 4]
        if dt != F32:
            ap = ap.bitcast(dt)
        if len(shape) == 2:
            ap = ap.rearrange("p (a b) -> p a b", a=shape[0])
        elif len(shape) == 3:
            ap = ap.rearrange("p (a b c) -> p a b c", a=shape[0], b=shape[1])
        return ap

    def VB(name, off, shape, dt):
        sz = 4 if dt == F32 else 2
        n = int(np.prod(shape)) * sz
        P.buf(name, off, off + n)
        return V(off, shape, dt)

    def mm(out, lhsT, rhs, start, stop, reads, writes):
        P.op("pe", lambda e: e.matmul(out, lhsT=lhsT, rhs=rhs, start=start, stop=stop), reads, writes)

    def tr(out, in_, reads, writes):
        P.op("pe", lambda e: e.transpose(out=out, in_=in_, identity=identb[:]), list(reads) + [("identb",)], writes)

    def act(out, in_, func, reads, writes, **kw):
        P.op("act", lambda e: e.activation(out=out, in_=in_, func=func, **kw), reads, writes)

    def ts(out, in0, s1, s2, op0, op1, reads, writes, accum=None, eng="dve"):
        if op1 is None:
            P.op(eng, lambda e: e.tensor_scalar(out=out, in0=in0, scalar1=s1, scalar2=None, op0=op0), reads, writes)
        elif accum is None:
            P.op(eng, lambda e: e.tensor_scalar(out=out, in0=in0, scalar1=s1, scalar2=s2, op0=op0, op1=op1), reads, writes)
        else:
            P.op(eng, lambda e: e.tensor_scalar(out=out, in0=in0, scalar1=s1, scalar2=s2, op0=op0, op1=op1,
                                                accum_out=accum), reads, writes)

    def stt(out, in0, scalar, in1, op0, op1, reads, writes, eng="dve"):
        P.op(eng, lambda e: e.scalar_tensor_tensor(out=out, in0=in0, scalar=scalar, in1=in1, op0=op0, op1=op1),
             reads, writes)

    def tt(out, in0, in1, op, reads, writes, eng="dve"):
        P.op(eng, lambda e: e.tensor_tensor(out=out, in0=in0, in1=in1, op=op), reads, writes)

    def cp(eng, out, in_, reads, writes):
        if eng == "act":
            P.op("act", lambda e: e.copy(out=out, in_=in_), reads, writes)
        else:
            P.op(eng, lambda e: e.tensor_copy(out=out, in_=in_), reads, writes)

    def ld(dst, src, key, queue="sp", reads=()):
        P.dma(queue, lambda e: e.dma_start(out=dst, in_=src), reads, [key])

    def ldw(dst, src, key):
        P.dma("pool", lambda e: e.dma_start(out=dst, in_=src), (), [key])

    evac_rot = Rot(["act", "dve"])
    dbg_names = []

    def dbg(name, ap, keys, shape, dt):
        if not debug:
            return
        d = nc.dram_tensor("dbg_" + name, [128] + list(shape), dt, kind="ExternalOutput").ap()
        P.buf("dbg_" + name)
        dbg_names.append("dbg_" + name)
        P.dma("sp", lambda e: e.dma_start(out=d, in_=ap), keys, [("dbg_" + name,)])

    for i in range(8):
        P.buf("ps%d" % i)
    P.buf("identb")
    P.buf("out")
    PSK = [("ps%d" % i,) for i in range(8)]

    SM = 196 * KB
    sm_off = [SM]

    def Vs(n, dt=F32):
        off = sm_off[0]
        sz = n * (4 if dt == F32 else 2)
        sz = (sz + 3) // 4 * 4
        sm_off[0] += sz
        assert sm_off[0] <= ARENA
        ap = arena[:, off // 4:(off + sz) // 4]
        if dt != F32:
            ap = ap.bitcast(dt)
        return ap

    P.buf("sm", SM, ARENA)
    NACC = 32 + 32 + 8 + 8 + 8 + 32 + 8 + NIT * 8
    acc = Vs(NACC)
    o = 0
    ssK = acc[:, o:o + 32]; o += 32
    ssc = acc[:, o:o + 32]; o += 32
    ssQ = acc[:, o:o + 8]; o += 8
    ss2 = acc[:, o:o + 8]; o += 8
    ssF = acc[:, o:o + 8]; o += 8
    vsum = acc[:, o:o + 32]; o += 32
    vsq = acc[:, o:o + 8]; o += 8
    cnt = acc[:, o:o + NIT * 8]; o += NIT * 8
    rsK = Vs(32); rsc = Vs(32); rsQ = Vs(8); rs2 = Vs(8); rsF = Vs(8)
    vmean = Vs(8); vvar = Vs(8); vrstd = Vs(8); vtmp = Vs(8)
    wsc = Vs(128)
    rmin = Vs(8); rmax = Vs(8); w0 = Vs(8); lo0 = Vs(8); mid = Vs(8); thr = Vs(8); tmpc = Vs(8)
    hw = Vs(8 * 32)
    pow2 = Vs(32)
    rcp = Vs(4)
    wsT = Vs(1024, BF16)
    svt = Vs(512).rearrange("p (a b) -> p a b", a=4)
    wsT3 = wsT.rearrange("p (a b) -> p a b", a=8)
    hw3 = hw.rearrange("p (a b) -> p a b", a=8)
    wsc3 = wsc.rearrange("p (a b) -> p a b", a=8)

    P.op("dve", lambda e: e.memset(acc, 0.0), (), [("sm", "acc")])
    ld(pow2, pow2_d, ("sm", "pow2"))
    ldw(identb[:], ident_d, ("identb",))

    def rstd_batch(ss_ap, rs_ap, inv_n, skeys, rkey):
        ts(rs_ap, ss_ap, inv_n, EPS, ALU.mult, ALU.add, skeys, [rkey])
        act(rs_ap, rs_ap, AF.Sqrt, [rkey], [rkey])
        P.op("dve", lambda e: e.reciprocal(out=rs_ap, in_=rs_ap), [rkey], [rkey])

    cT = VB("cT", 0, [4, 4096], BF16)
    xg = [VB("xg%d" % s, 32 * KB + s * 16 * KB, [16, 512], BF16) for s in range(2)]
    Wk = VB("Wk", 64 * KB, [16, 640], BF16)
    kvg = VB("kvg", 84 * KB, [512], F32)
    cn = [VB("cn%d" % s, 86 * KB + s * KB, [512], BF16) for s in range(2)]
    xn = [VB("xn%d" % s, 88 * KB + s * 4 * KB, [2048], BF16) for s in range(2)]
    kx = VB("kx", 96 * KB, [4096], BF16)
    qx = VB("qx", 104 * KB, [8, 1024], BF16)
    sqj = VB("sqj", 120 * KB, [2048], F32)
    xt = [VB("xt%d" % s, 156 * KB + s * 8 * KB, [2048], F32) for s in range(4)]
    g1 = VB("g1", 188 * KB, [2048], F32)

    ld(g1, g1bc_d, ("g1",))
    ld(kvg, kvgbc_d, ("kvg",))
    ldw(Wk[:, :, 0:512], win[:, :, C_CKV:C_CKV + 512], ("Wk", 0))
    ldw(Wk[:, :, 512:576], win[:, :, C_KI:C_KI + 64], ("Wk", 1))
    ldw(Wk[:, :, 576:640], win[:, :, C_KI:C_KI + 64], ("Wk", 2))
    WKK = [("Wk", 0), ("Wk", 1), ("Wk", 2)]

    front_evac = [None, None]

    def front_pre(src_d, row0, ss_ap, rs_ap, sbase):
        for tt_ in range(4):
            ld(xt[tt_], src_d[row0 + tt_ * 128: row0 + (tt_ + 1) * 128, :], ("xt%d" % tt_,))
        rkeys = []
        for hf in range(2):
            for tt_ in (2 * hf, 2 * hf + 1):
                c = sbase + tt_
                act(sqj, xt[tt_], AF.Square, [("xt%d" % tt_,), ("sm", "acc")], [("sqj",), ("sm", "ss", id(ss_ap), c)],
                    accum_out=ss_ap[:, c:c + 1])
            rkey = ("sm", "rs", id(rs_ap), sbase + 2 * hf)
            c0 = sbase + 2 * hf
            rstd_batch(ss_ap[:, c0:c0 + 2], rs_ap[:, c0:c0 + 2], 1.0 / 2048,
                       [("sm", "ss", id(ss_ap), c0 + k) for k in range(2)], rkey)
            rkeys.append(rkey)
        return rkeys

    def front_stt(tt_, rs_ap, sbase, rkey, gbc, gbckey):
        c = sbase + tt_
        s = tt_ % 2
        stt(xn[s], xt[tt_], rs_ap[:, c:c + 1], gbc, ALU.mult, ALU.mult,
            [("xt%d" % tt_,), rkey[tt_ // 2], gbckey], [("xn%d" % s,)])

    def front_te(tt_, dst_fn, dkey_fn):
        s = tt_ % 2
        for kc in range(16):
            tr(psb[kc // 8][:, (kc % 8) * 128:(kc % 8 + 1) * 128], xn[s][:, kc * 128:(kc + 1) * 128],
               [("xn%d" % s,)], [PSK[kc // 8]])
        for half in range(2):
            cp(front_evac[half] if front_evac[0] else evac_rot.next(), dst_fn(tt_, half),
               psb[half][:, 0:1024].rearrange("p (a b) -> p a b", a=8), [PSK[half]], [dkey_fn(tt_)])

    def front_tile(tt_, rs_ap, sbase, rkey, dst_fn, dkey_fn, gbc, gbckey):
        front_stt(tt_, rs_ap, sbase, rkey, gbc, gbckey)
        front_te(tt_, dst_fn, dkey_fn)

    def front(src_d, row0, ss_ap, rs_ap, sbase, gkey_fn, dst_fn, dkey_fn, gbc, gbckey):
        rkey = front_pre(src_d, row0, ss_ap, rs_ap, sbase)
        front_stt(0, rs_ap, sbase, rkey, gbc, gbckey)
        for tt_ in range(4):
            if tt_ + 1 < 4:
                front_stt(tt_ + 1, rs_ap, sbase, rkey, gbc, gbckey)
            front_te(tt_, dst_fn, dkey_fn)

    def back_ckv(g, tt_):
        gs = g % 2
        T = 4 * g + tt_
        b = 2 + tt_
        for kc in range(16):
            mm(ps[b][:, :], xg[gs][:, kc, tt_ * 128:(tt_ + 1) * 128], Wk[:, kc, 0:512], kc == 0, kc == 15,
               [("xg%d" % gs, tt_)] + WKK, [PSK[b]])
        act(sqj[:, 0:512], ps[b][:, :], AF.Square, [PSK[b], ("sm", "acc")], [("sqj",), ("sm", "ssc", T)],
            accum_out=ssc[:, T:T + 1])

    def c_rstd(g):
        rkey = ("sm", "rsc", g)
        rstd_batch(ssc[:, 4 * g:4 * g + 4], rsc[:, 4 * g:4 * g + 4], 1.0 / 512,
                   [("sm", "ssc", 4 * g + k) for k in range(4)], rkey)

    def post_pre(g):
        gs = g % 2
        rkey = ("sm", "rsc", g)
        for kc in range(16):
            mm(ps[7][:, :], Wk[:, kc, 512:640], xg[gs][:, kc, :], kc == 0, kc == 15,
               [("xg%d" % gs, t4) for t4 in range(4)] + WKK, [PSK[7]])
        cp("dve", kx[:, g * 512:(g + 1) * 512], ps[7][:, :], [PSK[7]], [("kx", g)])
        return rkey

    def post_tile(g, tt_, rkey):
        T = 4 * g + tt_
        b = 2 + tt_
        s = T % 2
        stt(cn[s], ps[b][:, :], rsc[:, T:T + 1], kvg, ALU.mult, ALU.mult,
            [PSK[b], rkey, ("kvg",)], [("cn%d" % s,)])
        for cc in range(4):
            tr(psb[6][:, cc * 128:(cc + 1) * 128], cn[s][:, cc * 128:(cc + 1) * 128], [("cn%d" % s,)], [PSK[6]])
        cp("dve", cT[:, :, T * 128:(T + 1) * 128],
           psb[6][:, 0:512].rearrange("p (a b) -> p a b", a=4), [PSK[6]], [("cT", g)])

    def kdst(gs):
        return (lambda tt_, half: xg[gs][:, half * 8:(half + 1) * 8, tt_ * 128:(tt_ + 1) * 128],
                lambda tt_: ("xg%d" % gs, tt_))

    def k_stats_emit(g):
        for tt_ in range(4):
            ld(xt[tt_], x_all[(4 * g + tt_) * 128:(4 * g + tt_ + 1) * 128, :], ("xt%d" % tt_,))
        for hf in range(2):
            for tt_ in (2 * hf, 2 * hf + 1):
                c_ = 4 * g + tt_
                act(sqj, xt[tt_], AF.Square, [("xt%d" % tt_,), ("sm", "acc")], [("sqj",), ("sm", "ss", id(ssK), c_)],
                    accum_out=ssK[:, c_:c_ + 1])
            c0 = 4 * g + 2 * hf
            act(rsK[:, c0:c0 + 2], ssK[:, c0:c0 + 2], AF.Sqrt, [("sm", "ss", id(ssK), c0 + k) for k in range(2)],
                [("sm", "rs", id(rsK), c0)], scale=1.0 / 2048, bias=EPS)

    def k_recip(g):
        keys = []
        for hf in range(2):
            c0 = 4 * g + 2 * hf
            key = ("sm", "rs", id(rsK), c0)
            P.op("dve", lambda e, c0=c0: e.reciprocal(out=rsK[:, c0:c0 + 2], in_=rsK[:, c0:c0 + 2]), [key], [key])
            keys.append(key)
        return keys

    front_evac[0], front_evac[1] = "act", "dve"
    k_stats_emit(0)
    rk = k_recip(0)
    d1, k1 = kdst(0)
    for tt_ in range(4):
        front_tile(tt_, rsK, 0, rk, d1, k1, g1, ("g1",))
    k_stats_emit(1)
    for g in range(9):
        prk = post_pre(g - 1) if g >= 1 else None

        def Pt(tt_):
            if g >= 1:
                post_tile(g - 1, tt_, prk)

        def Ct(tt_):
            if g < 8:
                back_ckv(g, tt_)

        def St(tt_):
            if g + 1 < 8:
                front_stt(tt_, rsK, 4 * (g + 1), rk, g1, ("g1",))

        def Te(tt_):
            if g + 1 < 8:
                front_te(tt_, d1, k1)
        Pt(0); Ct(0); Pt(1)
        if g + 1 < 8:
            rk = k_recip(g + 1)
            d1, k1 = kdst((g + 1) % 2)
        St(0); Ct(1); Te(0); Pt(2); St(1); Ct(2); Te(1); Pt(3); St(2); Ct(3); Te(2); St(3); Te(3)
        if g < 8:
            c_rstd(g)
        if g + 2 < 8:
            k_stats_emit(g + 2)
    front_evac[0], front_evac[1] = None, None

    dbg("cT", cT, [("cT", g) for g in range(8)], [4, 4096], BF16)
    dbg("kx", kx, [("kx", g) for g in range(8)], [4096], BF16)
    xnT = VB("xnT", 32 * KB, [16, 1024], BF16)
    XNT = [("xnT", T) for T in range(8)]
    for g in range(2):
        front(x_own, g * 512, ssQ, rsQ, 4 * g, None,
              lambda tt_, half, g=g: xnT[:, half * 8:(half + 1) * 8, (4 * g + tt_) * 128:(4 * g + tt_ + 1) * 128],
              lambda tt_, g=g: ("xnT", 4 * g + tt_), g1, ("g1",))

    Wp = [VB("Wp%d" % s, 64 * KB + s * 16 * KB, [16, 512], BF16) for s in range(2)]
    Wwi = VB("Wwi", 156 * KB, [16, 16], BF16)
    prb = Rot([0, 1, 2, 3])
    for pn in range(2):
        ldw(Wp[pn], win[:, :, C_QI + pn * 512:C_QI + (pn + 1) * 512], ("Wp%d" % pn,))
        for c in range(4):
            hp = pn * 4 + c
            for th in range(2):
                b = prb.next()
                for kc in range(16):
                    mm(ps[b][:, :], Wp[pn][:, kc, c * 128:(c + 1) * 128], xnT[:, kc, th * 512:(th + 1) * 512],
                       kc == 0, kc == 15, [("Wp%d" % pn,)] + XNT, [PSK[b]])
                cp(evac_rot.next(), qx[:, hp, th * 512:(th + 1) * 512], ps[b][:, :], [PSK[b]], [("qx", hp)])
    ldw(Wwi, win[:, :, C_WI:C_WI + 16], ("Wwi",))
    for T in range(8):
        b = prb.next()
        for kc in range(16):
            mm(ps[b][:, 0:16], xnT[:, kc, T * 128:(T + 1) * 128], Wwi[:, kc, :], kc == 0, kc == 15,
               [("Wwi",)] + XNT, [PSK[b]])
        ts(wsc3[:, T, :], ps[b][:, 0:16], 1.0 / 32.0, None, ALU.mult, None, [PSK[b]], [("sm", "wsc", T)])

    dbg("xnT", xnT, XNT, [16, 1024], BF16)
    dbg("qx", qx, [("qx", hp) for hp in range(8)], [8, 1024], BF16)
    dbg("wsc", wsc, [("sm", "wsc", T) for T in range(8)], [128], F32)
    score = [VB("score%d" % s, 64 * KB + s * 16 * KB, [4096], F32) for s in range(2)]
    MT = VB("MT", 120 * KB, [144, 128], BF16)
    Rs = [VB("R%d" % s, 156 * KB + s * KB, [512], BF16) for s in range(4)]
    admb = VB("admb", 160 * KB, [512], F32)
    Dm = [VB("Dm%d" % s, 162 * KB + s * 4 * KB, [16, 128], BF16) for s in range(2)]
    Mtm = VB("Mtm", 170 * KB, [4096], BF16)
    absw = [VB("absw%d" % s, 178 * KB + s * 64, [16], F32) for s in range(2)]
    sgnh = [VB("sgnh%d" % s, 178 * KB + 128 + s * 64, [16], F32) for s in range(2)]
    ld(admb, admb_d, ("admb",))
    kxB = VB("kxB", 180 * KB, [4096], BF16)
    KXA = [("kx", g) for g in range(8)]
    P.op("dve", lambda e: e.memset(kxB[0:64, :], 0.0), (), [("kxB", 0)])
    cp("dve", kxB[64:128, :], kx[64:128, :], KXA, [("kxB", 1)])
    P.op("dve", lambda e: e.memset(kx[64:128, :], 0.0), KXA, KXA)
    lb = Rot([0, 1, 2, 3])
    rr = Rot([0, 1, 2, 3])
    sbr = Rot([4, 5])
    tb = Rot([6, 7])

    def scores(i):
        s2 = i % 2
        sc = score[s2]
        ts(sgnh[s2], wsc3[:, i, :], 0.0, 0.5, ALU.is_gt, ALU.subtract, [("sm", "wsc", i)], [("sgnh%d" % s2,)])
        stt(absw[s2], wsc3[:, i, :], 4.0, sgnh[s2], ALU.mult, ALU.mult, [("sm", "wsc", i), ("sgnh%d" % s2,)], [("absw%d" % s2,)])
        tt(Dm[s2], identb[:].unsqueeze(1).to_broadcast([128, 16, 128]),
           sgnh[s2].unsqueeze(2).to_broadcast([128, 16, 128]), ALU.mult,
           [("identb",), ("sgnh%d" % s2,)], [("Dm%d" % s2,)])
        items = [(sg, h) for sg in range(i + 1) for h in range(16)]
        lbank = {}

        def emitL(n):
            sg, h = items[n]
            hp, base = h // 2, (h % 2) * 64
            bnk = lb.next()
            lbank[n] = bnk
            kk_ = kx if base == 0 else kxB
            mm(ps[bnk][:, :], qx[:, hp, i * 128:(i + 1) * 128], kk_[:, sg * 512:(sg + 1) * 512],
               True, True, [("qx", hp), ("kx", sg), ("kxB", 0), ("kxB", 1)], [PSK[bnk]])
        LA = 2
        for n in range(min(LA, len(items))):
            emitL(n)
        sb = None
        for n in range(len(items)):
            sg, h = items[n]
            if n + LA < len(items):
                emitL(n + LA)
            bnk = lbank.pop(n)
            r = rr.next()
            act(Rs[r], ps[bnk][:, :], AF.Relu, [PSK[bnk], ("absw%d" % s2,)], [("R%d" % r,)], scale=absw[s2][:, h:h + 1])
            if h == 0:
                sb = sbr.next()
            mm(ps[sb][:, :], Dm[s2][:, h, :], Rs[r], h == 0, h == 15, [("Dm%d" % s2,), ("R%d" % r,)], [PSK[sb]])
            if h == 15:
                cp("act", sc[:, sg * 512:(sg + 1) * 512], ps[sb][:, :], [PSK[sb]], [("score%d" % s2, sg)])

    def bisect(i):
        s2 = i % 2
        sc = score[s2]
        nk = 512 * (i + 1)
        SC = [("score%d" % s2, sg) for sg in range(i + 1)]
        bk = ("sm", "bis", i)
        P.op("dve", lambda e: e.tensor_reduce(out=rmin[:, i:i + 1], in_=sc[:, 0:nk], axis=AX.X, op=ALU.min), SC, [bk])
        tt(sc[:, i * 512:(i + 1) * 512], sc[:, i * 512:(i + 1) * 512], admb, ALU.add,
           [("score%d" % s2, i), ("admb",)], [("score%d" % s2, i)])
        P.op("dve", lambda e: e.reduce_max(out=rmax[:, i:i + 1], in_=sc[:, 0:nk], axis=AX.X), SC + [bk], [bk])
        ts(lo0[:, i:i + 1], rmin[:, i:i + 1], -1.0, None, ALU.add, None, [bk], [bk])
        stt(w0[:, i:i + 1], rmax[:, i:i + 1], 1.0, rmin[:, i:i + 1], ALU.add, ALU.subtract, [bk], [bk])
        ts(hw3[:, i, :], pow2, w0[:, i:i + 1], None, ALU.mult, None, [bk, ("sm", "pow2")], [bk])
        tt(mid[:, i:i + 1], lo0[:, i:i + 1], hw3[:, i, 0:1], ALU.add, [bk], [bk])
        for k in range(NIT):
            cc = i * NIT + k
            ts(Mtm[:, 0:nk], sc[:, 0:nk], mid[:, i:i + 1], 0.0, ALU.is_gt, ALU.add,
               SC + [bk, ("sm", "acc")], [("Mtm",), bk], accum=cnt[:, cc:cc + 1])
            ts(tmpc[:, i:i + 1], cnt[:, cc:cc + 1], 256.0, 0.5, ALU.is_ge, ALU.subtract, [bk], [bk])
            stt(mid[:, i:i + 1], tmpc[:, i:i + 1], hw3[:, i, k:k + 1], mid[:, i:i + 1], ALU.mult, ALU.add, [bk], [bk])
        tt(thr[:, i:i + 1], mid[:, i:i + 1], hw3[:, i, NIT:NIT + 1], ALU.subtract, [bk], [bk])
        ts(Mtm[:, 0:nk], sc[:, 0:nk], thr[:, i:i + 1], None, ALU.is_gt, None, SC + [bk], [("Mtm",)])
        off = 2 * i * (i + 1)
        nkt = 4 * (i + 1)
        for k0 in range(0, nkt, 8):
            n = min(8, nkt - k0)
            bnk = tb.next()
            for kk in range(n):
                kt = k0 + kk
                tr(psb[bnk][:, kk * 128:(kk + 1) * 128], Mtm[:, kt * 128:(kt + 1) * 128], [("Mtm",)], [PSK[bnk]])
            cp("act", MT[:, off + k0:off + k0 + n, :],
               psb[bnk][:, 0:n * 128].rearrange("p (a b) -> p a b", a=n), [PSK[bnk]], [("MT", i)])

    scores(0)
    for i in range(8):
        if i + 1 < 8:
            scores(i + 1)
        bisect(i)

    dbg("thr", thr, [("sm", "bis", i) for i in range(8)], [8], F32)
    dbg("rmin", rmin, [("sm", "bis", i) for i in range(8)], [8], F32)
    dbg("rmax", rmax, [("sm", "bis", i) for i in range(8)], [8], F32)
    dbg("cnt", cnt, [("sm", "bis", i) for i in range(8)], [NIT * 8], F32)
    dbg("score7", score[1], [("score1", sg) for sg in range(8)], [4096], F32)
    dbg("MT", MT, [("MT", i) for i in range(8)], [144, 128], BF16)
    ybT = VB("ybT", 64 * KB, [16, 1024], BF16)
    KT = VB("KT", 96 * KB, [2, 4096], BF16)
    Wq = VB("Wq", 112 * KB, [16, 256], BF16)
    Vaug = VB("Vaug", 156 * KB, [32, 2, 129], BF16)
    o_ = 156 * KB + 16512
    wuk_s = VB("wuk_s", o_, [4, 256], BF16); o_ += 2 * KB
    wuv_s = VB("wuv_s", o_, [4, 256], BF16); o_ += 2 * KB
    qT = VB("qT", o_, [2, 1024], BF16); o_ += 4 * KB
    Es = []
    for s in range(3):
        Es.append(VB("E%d" % s, o_, [512], BF16)); o_ += KB
    PTs = []
    for s in range(3):
        PTs.append(VB("PT%d" % s, o_, [512], BF16)); o_ += KB
    Ons = []
    for s in range(2):
        Ons.append(VB("On%d" % s, o_, [128], BF16)); o_ += 256
    assert o_ <= 196 * KB
    P.op("dve", lambda e: e.memset(Vaug[:, :, :, 128:129], 1.0), (), [("Vaug", "ones")])
    CT = [("cT", g) for g in range(8)]
    SCALE = float(128 ** -0.5)
    pjb = Rot([0, 1, 2, 6, 7])
    trb = Rot([7])
    sbk = Rot([0, 1, 2, 6])
    esr = Rot([0, 1, 2])
    onr = Rot([0, 1])
    blk_ctr = [0]
    for hg in range(8):
        ldw(wuk_s, wuk[:, :, hg * 256:(hg + 1) * 256], ("wuk_s",))
        ldw(wuv_s, wuv[:, :, hg * 256:(hg + 1) * 256], ("wuv_s",))
        ldw(Wq, win[:, :, C_Q + hg * 256:C_Q + (hg + 1) * 256], ("Wq",))
        for hh in range(2):
            for th in range(2):
                b = pjb.next()
                for kc in range(16):
                    mm(ps[b][:, :], Wq[:, kc, hh * 128:(hh + 1) * 128], xnT[:, kc, th * 512:(th + 1) * 512],
                       kc == 0, kc == 15, [("Wq",)] + XNT, [PSK[b]])
                cp(evac_rot.next(), qT[:, hh, th * 512:(th + 1) * 512], ps[b][:, :], [PSK[b]], [("qT", hh)])
        for hh in range(2):
            for sg in range(8):
                b = pjb.next()
                for cc in range(4):
                    mm(ps[b][:, :], wuk_s[:, cc, hh * 128:(hh + 1) * 128], cT[:, cc, sg * 512:(sg + 1) * 512],
                       cc == 0, cc == 3, [("wuk_s",), ("cT", sg)], [PSK[b]])
                cp(evac_rot.next(), KT[:, hh, sg * 512:(sg + 1) * 512], ps[b][:, :], [PSK[b]], [("KT", hh, sg)])
        for st2 in range(16):
            b = pjb.next()
            for j in range(2):
                stile = 2 * st2 + j
                for cc in range(4):
                    mm(ps[b][:, j * 256:(j + 1) * 256], cT[:, cc, stile * 128:(stile + 1) * 128], wuv_s[:, cc, :],
                       cc == 0, cc == 3, [("wuv_s",), ("cT", stile // 4)], [PSK[b]])
            cp(evac_rot.next(), Vaug[:, 2 * st2:2 * st2 + 2, :, 0:128],
               ps[b][:, :].rearrange("p (a b c) -> p a b c", a=2, b=2), [PSK[b]], [("Vaug", st2 // 2)])
        if hg < 0:
            dbg("KT%d" % hg, KT, [("KT", hh, sg) for hh in range(2) for sg in range(8)], [2, 4096], BF16)
            dbg("Vaug%d" % hg, Vaug, [("Vaug", q) for q in range(8)] + [("Vaug", "ones")], [32, 2, 129], BF16)
            dbg("qT%d" % hg, qT, [("qT", 0), ("qT", 1)], [2, 1024], BF16)
            dbg("wuv%d" % hg, wuv_s, [("wuv_s",)], [4, 256], BF16)
        groups = [(hh, i, sgi) for hh in range(2) for i in range(8) for sgi in range(i + 1)]
        sbank = {}

        def emitS(n):
            hh, i, sgi = groups[n]
            b = sbk.next()
            sbank[n] = b
            for k4 in range(4):
                kt = 4 * sgi + k4
                mm(ps[b][:, k4 * 128:(k4 + 1) * 128], KT[:, hh, kt * 128:(kt + 1) * 128], qT[:, hh, i * 128:(i + 1) * 128],
                   True, True, [("KT", hh, sgi), ("qT", hh)], [PSK[b]])

        deferred = []

        def run_deferred(upto):
            deferred.sort(key=lambda t: (t[0], t[1]))
            while deferred and deferred[0][0] <= upto:
                deferred.pop(0)[2]()

        seq = [0]

        def schedule_final(n, hh, i, ob):
            s = onr.next()

            def f_dve():
                P.op("dve", lambda e: e.reciprocal(out=rcp[:, 0:1], in_=ps[ob][:, 128:129]), [PSK[ob]], [("sm", "rcp")])
                ts(Ons[s], ps[ob][:, 0:128], rcp[:, 0:1], None, ALU.mult, None, [PSK[ob], ("sm", "rcp")], [("On%d" % s,)])
            tbank = [None]

            def f_tr():
                tbank[0] = trb.next()
                tr(psb[tbank[0]][:, 0:128], Ons[s], [("On%d" % s,)], [PSK[tbank[0]]])

            def f_ev():
                cp("act", ybT[:, 2 * hg + hh, i * 128:(i + 1) * 128], psb[tbank[0]][:, 0:128], [PSK[tbank[0]]],
                   [("ybT", 2 * hg + hh, i)])
            for due, fn in ((n + 2, f_dve), (n + 3, f_tr), (n + 4, f_ev)):
                seq[0] += 1
                deferred.append((due, seq[0], fn))

        emitS(0)
        emitS(1)
        emitS(2)
        for n in range(len(groups)):
            hh, i, sgi = groups[n]
            if n + 3 < len(groups):
                emitS(n + 3)
            b = sbank.pop(n)
            es = esr.next()
            act(Es[es], ps[b][:, :], AF.Exp, [PSK[b]], [("E%d" % es,)], scale=SCALE)
            off = 2 * i * (i + 1)
            tt(PTs[es], Es[es], MT[:, off + 4 * sgi:off + 4 * sgi + 4, :].rearrange("p a b -> p (a b)"), ALU.mult,
               [("E%d" % es,), ("MT", i)], [("PT%d" % es,)])
            if sgi == 0:
                blk_ctr[0] += 1
            ob = 3 + (blk_ctr[0] % 3)
            for k4 in range(4):
                kt = 4 * sgi + k4
                mm(ps[ob][:, 0:129], PTs[es][:, k4 * 128:(k4 + 1) * 128], Vaug[:, kt, hh, :],
                   (sgi == 0 and k4 == 0), (sgi == i and k4 == 3),
                   [("PT%d" % es,), ("Vaug", kt // 4), ("Vaug", "ones")], [PSK[ob]])
            run_deferred(n)
            if sgi == i:
                schedule_final(n, hh, i, ob)
        run_deferred(10 ** 9)

    dbg("ybT", ybT, [("ybT", h, i) for h in range(16) for i in range(8)], [16, 1024], BF16)
    yaT = VB("yaT", 96 * KB, [16, 1024], BF16)
    vg = VB("vg", 128 * KB, [8, 2048], BF16)
    lng = VB("lng", 160 * KB, [2048], F32)
    lnb = VB("lnb", 168 * KB, [2048], F32)
    vln = [VB("vln0", 176 * KB, [2048], BF16)]
    vtf = VB("vtf", 180 * KB, [2048], F32)
    bsb = VB("bsb", 188 * KB, [16, 128], F32)
    WA = [VB("WA%d" % s, s * 16 * KB, [16, 512], BF16) for s in range(2)]
    ld(lng, lngbc_d, ("lng",))
    ld(lnb, lnbbc_d, ("lnb",))
    ld(bsb, bsbc_d.rearrange("p (a b) -> p a b", a=16), ("bsb",))
    ldw(wsT3, wsT_d.rearrange("p (a b) -> p a b", a=8), ("sm", "wsT"))
    P.op("dve", lambda e: e.memset(wsT3[64:128, :, 0:64], 0.0), [("sm", "wsT")], [("sm", "wsT")])
    war = Rot([0, 1])
    ab = Rot([0, 1, 2, 3])
    mb = Rot([4, 5, 6, 7])
    for pn in range(4):
        s = war.next()
        ldw(WA[s], win[:, :, C_V + pn * 512:C_V + (pn + 1) * 512], ("WA%d" % s,))
        for T in range(8):
            b = ab.next()
            for kc in range(16):
                mm(ps[b][:, :], xnT[:, kc, T * 128:(T + 1) * 128], WA[s][:, kc, :], kc == 0, kc == 15,
                   [("WA%d" % s,)] + XNT, [PSK[b]])
            cidx = T * 4 + pn
            act(vg[:, T, pn * 512:(pn + 1) * 512], ps[b][:, :], AF.Gelu_apprx_tanh, [PSK[b], ("sm", "acc")],
                [("vg", T, pn), ("sm", "vsum", cidx)], accum_out=vsum[:, cidx:cidx + 1])
    for T in range(8):
        VG = [("vg", T, pn) for pn in range(4)]
        act(vtf, vg[:, T, :], AF.Square, VG + [("sm", "acc")], [("vtf",), ("sm", "vsq", T)], accum_out=vsq[:, T:T + 1])
    sk = ("sm", "vstat")
    P.op("dve", lambda e: e.reduce_sum(out=vmean[:, 0:8], in_=vsum[:, 0:32].rearrange("p (a b) -> p a b", b=4), axis=AX.X),
         [("sm", "vsum", c) for c in range(32)], [sk])
    ts(vmean[:, 0:8], vmean[:, 0:8], 1.0 / 2048, None, ALU.mult, None, [sk], [sk])
    tt(vtmp[:, 0:8], vmean[:, 0:8], vmean[:, 0:8], ALU.mult, [sk], [sk])
    stt(vvar[:, 0:8], vsq[:, 0:8], 1.0 / 2048, vtmp[:, 0:8], ALU.mult, ALU.subtract,
        [sk] + [("sm", "vsq", T) for T in range(8)], [sk])
    ts(vrstd[:, 0:8], vvar[:, 0:8], EPS, None, ALU.add, None, [sk], [sk])
    act(vrstd[:, 0:8], vrstd[:, 0:8], AF.Sqrt, [sk], [sk])
    P.op("dve", lambda e: e.reciprocal(out=vrstd[:, 0:8], in_=vrstd[:, 0:8]), [sk], [sk])
    stt(vtmp[:, 0:8], vmean[:, 0:8], -1.0, vrstd[:, 0:8], ALU.mult, ALU.mult, [sk], [sk])
    for pn in range(4):
        s = war.next()
        ldw(WA[s], win[:, :, C_U + pn * 512:C_U + (pn + 1) * 512], ("WA%d" % s,))
        for c in range(4):
            ch = pn * 4 + c
            for th in range(2):
                b = ab.next()
                for kc in range(16):
                    mm(ps[b][:, :], WA[s][:, kc, c * 128:(c + 1) * 128], xnT[:, kc, th * 512:(th + 1) * 512],
                       kc == 0, kc == 15, [("WA%d" % s,)] + XNT, [PSK[b]])
                act(yaT[:, ch, th * 512:(th + 1) * 512], ps[b][:, :], AF.Gelu_apprx_tanh, [PSK[b]], [("yaT", ch, th)])
    for T in range(8):
        VG = [("vg", T, pn) for pn in range(4)]
        act(vtf, vg[:, T, :], AF.Identity, VG + [sk], [("vtf",)], scale=vrstd[:, T:T + 1], bias=vtmp[:, T:T + 1])
        tt(vtf, vtf, lng, ALU.mult, [("vtf",), ("lng",)], [("vtf",)])
        tt(vln[0], vtf, lnb, ALU.add, [("vtf",), ("lnb",)], [("vln0",)])
        for cb in range(4):
            b = mb.next()
            for c4 in range(4):
                ch = cb * 4 + c4
                mm(ps[b][:, c4 * 128:(c4 + 1) * 128], vln[0][:, ch * 128:(ch + 1) * 128], wsT3[:, ch // 2, :],
                   True, True, [("vln0",), ("sm", "wsT")], [PSK[b]])
            ysl = yaT[:, cb * 4:(cb + 1) * 4, T * 128:(T + 1) * 128]
            YK = [("yaT", cb * 4 + c4, T // 4) for c4 in range(4)]
            tt(svt, ps[b][:, :].rearrange("p (a b) -> p a b", a=4), bsb[:, cb * 4:(cb + 1) * 4, :], ALU.add,
               [PSK[b], ("bsb",)], [("sm", "svt")])
            tt(ysl, svt, ysl, ALU.mult, [("sm", "svt")] + YK, YK)

    dbg("yaT", yaT, [("yaT", ch, th) for ch in range(16) for th in range(2)], [16, 1024], BF16)
    mixT = VB("mixT", 0, [16, 1024], BF16)
    WM = {}
    o_ = 128 * KB
    for nm in ("oa", "ob", "ga", "gb"):
        for s in range(2):
            WM[(nm, s)] = VB("WM%s%d" % (nm, s), o_, [16, 256], BF16)
            o_ += 8 * KB
    gaf = VB("gaf", 192 * KB, [512], F32)
    gbf = VB("gbf", 194 * KB, [512], F32)
    YA = [("yaT", ch, th) for ch in range(16) for th in range(2)]
    YB = [("ybT", h, i) for h in range(16) for i in range(8)]
    xb_ = Rot([0, 1, 2, 3, 4, 5, 6, 7])
    for pn in range(8):
        s = pn % 2
        ldw(WM[("oa", s)], woa[:, :, pn * 256:(pn + 1) * 256], ("WMoa%d" % s,))
        ldw(WM[("ob", s)], wob[:, :, pn * 256:(pn + 1) * 256], ("WMob%d" % s,))
        ldw(WM[("ga", s)], win[:, :, C_GA + pn * 256:C_GA + (pn + 1) * 256], ("WMga%d" % s,))
        ldw(WM[("gb", s)], win[:, :, C_GB + pn * 256:C_GB + (pn + 1) * 256], ("WMgb%d" % s,))
        for c in range(2):
            n = pn * 2 + c
            for th in range(2):
                bA, bB, bGA, bGB = xb_.next(), xb_.next(), xb_.next(), xb_.next()
                tsl = slice(th * 512, (th + 1) * 512)
                for kc in range(16):
                    mm(ps[bGA][:, :], WM[("ga", s)][:, kc, c * 128:(c + 1) * 128], xnT[:, kc, tsl], kc == 0, kc == 15,
                       [("WMga%d" % s,)] + XNT, [PSK[bGA]])
                act(gaf, ps[bGA][:, :], AF.Sigmoid, [PSK[bGA]], [("gaf",)])
                for kc in range(16):
                    mm(ps[bGB][:, :], WM[("gb", s)][:, kc, c * 128:(c + 1) * 128], xnT[:, kc, tsl], kc == 0, kc == 15,
                       [("WMgb%d" % s,)] + XNT, [PSK[bGB]])
                act(gbf, ps[bGB][:, :], AF.Sigmoid, [PSK[bGB]], [("gbf",)])
                for kc in range(16):
                    mm(ps[bB][:, :], WM[("ob", s)][:, kc, c * 128:(c + 1) * 128], ybT[:, kc, tsl], kc == 0, kc == 15,
                       [("WMob%d" % s,)] + YB, [PSK[bB]])
                tt(gbf, ps[bB][:, :], gbf, ALU.mult, [PSK[bB], ("gbf",)], [("gbf",)])
                for kc in range(16):
                    mm(ps[bA][:, :], WM[("oa", s)][:, kc, c * 128:(c + 1) * 128], yaT[:, kc, tsl], kc == 0, kc == 15,
                       [("WMoa%d" % s,)] + [("yaT", ch, th) for ch in range(16)], [PSK[bA]])
                tt(gaf, ps[bA][:, :], gaf, ALU.mult, [PSK[bA], ("gaf",)], [("gaf",)])
                tt(mixT[:, n, tsl], gaf, gbf, ALU.add, [("gaf",), ("gbf",)], [("mixT", n, th)])

    dbg("mixT", mixT, [("mixT", n, th) for n in range(16) for th in range(2)], [16, 1024], BF16)
    hT = VB("h", 96 * KB, [8, 2048], F32)
    WO = [VB("WO%d" % s, 160 * KB + s * 16 * KB, [16, 512], BF16) for s in range(2)]
    xs = [VB("xs%d" % s, 192 * KB + s * 2 * KB, [512], F32) for s in range(2)]
    g2 = VB("g2", 64 * KB, [2048], F32)
    xn2 = [VB("xn2%d" % s, 72 * KB + s * 4 * KB, [2048], BF16) for s in range(2)]
    sqj3 = VB("sqj3", 80 * KB, [2048], F32)
    hnT = VB("hnT", 32 * KB, [16, 1024], BF16)
    ld(g2, g2bc_d, ("g2",))
    MX = [("mixT", n, th) for n in range(16) for th in range(2)]
    ob_ = Rot([2, 3, 4, 5])
    xr = Rot([0, 1])
    for np_ in range(4):
        s = np_ % 2
        ldw(WO[s], wout[:, :, np_ * 512:(np_ + 1) * 512], ("WO%d" % s,))
        for T in range(8):
            b = ob_.next()
            xsl = xr.next()
            ld(xs[xsl], x_own[T * 128:(T + 1) * 128, np_ * 512:(np_ + 1) * 512], ("xs%d" % xsl,))
            for kc in range(16):
                mm(ps[b][:, :], mixT[:, kc, T * 128:(T + 1) * 128], WO[s][:, kc, :], kc == 0, kc == 15,
                   [("WO%d" % s,)] + MX, [PSK[b]])
            tt(hT[:, T, np_ * 512:(np_ + 1) * 512], ps[b][:, :], xs[xsl], ALU.add, [PSK[b], ("xs%d" % xsl,)], [("h", T, np_)])
            if np_ == 3:
                act(sqj3, hT[:, T, :], AF.Square, [("h", T, q) for q in range(4)] + [("sm", "acc")],
                    [("sqj3",), ("sm", "ss2", T)], accum_out=ss2[:, T:T + 1])

    def norm_from_h(ss_ap, rs_ap, tag, jk, jkey):
        rkey = ("sm", tag + "r")
        rstd_batch(ss_ap[:, 0:8], rs_ap[:, 0:8], 1.0 / 2048, [("sm", tag, T) for T in range(8)], rkey)
        return rkey

    dbg("h1", hT, [("h", T, q) for T in range(8) for q in range(4)], [8, 2048], F32)
    rkey2 = norm_from_h(ss2, rs2, "ss2", sqj3, ("sqj3",))
    def hn_stt(T):
        HK = [("h", T, q) for q in range(4)]
        s = T % 2
        stt(xn2[s], hT[:, T, :], rs2[:, T:T + 1], g2, ALU.mult, ALU.mult, HK + [rkey2, ("g2",)], [("xn2%d" % s,)])

    hn_stt(0)
    for T in range(8):
        s = T % 2
        if T + 1 < 8:
            hn_stt(T + 1)
        for kc in range(16):
            tr(psb[kc // 8][:, (kc % 8) * 128:(kc % 8 + 1) * 128], xn2[s][:, kc * 128:(kc + 1) * 128],
               [("xn2%d" % s,)], [PSK[kc // 8]])
        for half in range(2):
            cp(evac_rot.next(), hnT[:, half * 8:(half + 1) * 8, T * 128:(T + 1) * 128],
               psb[half][:, 0:1024].rearrange("p (a b) -> p a b", a=8), [PSK[half]], [("hnT", T)])

    actT = VB("actT", 0, [12, 1024], BF16)
    WG = [VB("WG%d" % s, 160 * KB + s * 8 * KB, [16, 256], BF16) for s in range(2)]
    WU = [VB("WU%d" % s, 176 * KB + s * 8 * KB, [16, 256], BF16) for s in range(2)]
    sgf = [VB("sgf%d" % s, 192 * KB + s * 2 * KB, [512], F32) for s in range(2)]
    WD = [VB("WD%d" % s, 64 * KB + s * 12 * KB, [12, 512], BF16) for s in range(2)]
    gf = VB("gf", 88 * KB, [2048], F32)
    ld(gf, gfbc_d, ("gf",))
    HN = [("hnT", T) for T in range(8)]
    fb = Rot([0, 1, 2, 3])
    db = Rot([4, 5, 6, 7])
    sgr = Rot([0, 1])
    wdr = Rot([0, 1])
    pidx = 0
    for grp, npan in enumerate((6, 6, 6, 4)):
        nf = npan * 2
        for pl in range(npan):
            s = pidx % 2
            c0 = pidx * 256
            ldw(WG[s], wfg[:, :, c0:c0 + 256], ("WG%d" % s,))
            ldw(WU[s], wfu[:, :, c0:c0 + 256], ("WU%d" % s,))
            for c in range(2):
                fl = pl * 2 + c
                for th in range(2):
                    bG, bU = fb.next(), fb.next()
                    tsl = slice(th * 512, (th + 1) * 512)
                    for kc in range(16):
                        mm(ps[bG][:, :], WG[s][:, kc, c * 128:(c + 1) * 128], hnT[:, kc, tsl], kc == 0, kc == 15,
                           [("WG%d" % s,)] + HN[4 * th:4 * th + 4], [PSK[bG]])
                    q = sgr.next()
                    act(sgf[q], ps[bG][:, :], AF.Silu, [PSK[bG]], [("sgf%d" % q,)])
                    for kc in range(16):
                        mm(ps[bU][:, :], WU[s][:, kc, c * 128:(c + 1) * 128], hnT[:, kc, tsl], kc == 0, kc == 15,
                           [("WU%d" % s,)] + HN[4 * th:4 * th + 4], [PSK[bU]])
                    tt(actT[:, fl, tsl], ps[bU][:, :], sgf[q], ALU.mult, [PSK[bU], ("sgf%d" % q,)], [("actT", fl, th)])
            pidx += 1
        f0 = (pidx - npan) * 2
        AK = [("actT", fl, th) for fl in range(nf) for th in range(2)]
        for np_ in range(4):
            s = wdr.next()
            ldw(WD[s][:, 0:nf, :], wfd[:, f0:f0 + nf, np_ * 512:(np_ + 1) * 512], ("WD%d" % s,))
            for T in range(8):
                b = db.next()
                for fl in range(nf):
                    mm(ps[b][:, :], actT[:, fl, T * 128:(T + 1) * 128], WD[s][:, fl, :], fl == 0, fl == nf - 1,
                       [("WD%d" % s,)] + AK, [PSK[b]])
                hsl = hT[:, T, np_ * 512:(np_ + 1) * 512]
                tt(hsl, ps[b][:, :], hsl, ALU.add, [PSK[b], ("h", T, np_)], [("h", T, np_)])
                if grp == 3 and np_ == 3:
                    if T == 0:
                        sqjF = VB("sqjF", 160 * KB, [2048], F32)
                    act(sqjF, hT[:, T, :], AF.Square, [("h", T, q) for q in range(4)] + [("sm", "acc")],
                        [("sqjF",), ("sm", "ssF", T)], accum_out=ssF[:, T:T + 1])

    dbg("hnT", hnT, HN, [16, 1024], BF16)
    dbg("h2", hT, [("h", T, q) for T in range(8) for q in range(4)], [8, 2048], F32)
    rkeyF = norm_from_h(ssF, rsF, "ssF", None, None)
    for T in range(8):
        HK = [("h", T, q) for q in range(4)]
        stt(hT[:, T, :], hT[:, T, :], rsF[:, T:T + 1], gf, ALU.mult, ALU.mult, HK + [rkeyF, ("gf",)], HK)
        P.dma("sp", lambda e, T=T: e.dma_start(out=out_d[T * 128:(T + 1) * 128, :], in_=hT[:, T, :]), HK, [("out", T)])

    P.emit(st)
    st.close()
    P.dbg_names = dbg_names
    return nc, P


_CACHE = {}


def kernel(x, norm1_g, w_in, a_ln_g, a_ln_b, a_w_s, a_b_s, kv_norm_g, w_uk, w_uv, w_oa, w_ob, w_out,
           norm2_g, w_ff_gate, w_ff_up, w_ff_down, final_g):
    f = np.float32
    x = np.asarray(x, f)
    if "nc" not in _CACHE:
        _CACHE["nc"] = build_program()[0]
    nc = _CACHE["nc"]

    def bc(v, n):
        return np.ascontiguousarray(np.broadcast_to(np.asarray(v, f).reshape(1, n), (128, n)))

    shared = {
        "ident": np.eye(128, dtype=f),
        "pow2": bc(np.array([2.0 ** -(k + 1) for k in range(32)], f), 32),
        "g1bc": bc(norm1_g[0], 2048),
        "kvgbc": bc(kv_norm_g[0], 512),
        "lngbc": bc(a_ln_g[0], 2048),
        "lnbbc": bc(a_ln_b[0], 2048),
        "bsbc": bc(np.repeat(np.asarray(a_b_s[0], f), 2, axis=0).reshape(-1), 2048),
        "wsT": np.ascontiguousarray(np.transpose(np.asarray(a_w_s[0], f), (2, 0, 1)).reshape(128, 1024)),
        "g2bc": bc(norm2_g[0], 2048),
        "gfbc": bc(final_g, 2048),
        "w_in": np.ascontiguousarray(np.asarray(w_in[0], f)),
        "w_uk": np.ascontiguousarray(np.asarray(w_uk[0], f).reshape(512, 2048)),
        "w_uv": np.ascontiguousarray(np.asarray(w_uv[0], f).reshape(512, 2048)),
        "w_oa": np.ascontiguousarray(np.asarray(w_oa[0], f)),
        "w_ob": np.ascontiguousarray(np.asarray(w_ob[0], f)),
        "w_out": np.ascontiguousarray(np.asarray(w_out[0], f)),
        "w_ff_gate": np.ascontiguousarray(np.asarray(w_ff_gate[0], f)),
        "w_ff_up": np.ascontiguousarray(np.asarray(w_ff_up[0], f)),
        "w_ff_down": np.ascontiguousarray(np.asarray(w_ff_down[0], f)),
    }
    in_maps = []
    for c in range(8):
        b, j = c // 4, c % 4
        xb = x[b]
        own = np.ascontiguousarray(xb.reshape(8, 4, 128, 2048)[:, j].reshape(1024, 2048))
        sp = np.arange(512)[None, :]
        r = np.arange(128)[:, None]
        adm = (sp < 128 * j + 64) | ((sp < 128 * j + 128) & (r >= 64))
        admb = np.where(adm, 0.0, -1e30).astype(f)
        m = dict(shared)
        m["x_all"] = np.ascontiguousarray(xb)
        m["x_own"] = own
        m["admb"] = admb
        in_maps.append(m)
    if _CACHE.get("return_maps"):
        return in_maps
    res = run_bass_kernel_spmd(nc, in_maps, core_ids=list(range(8)))
    out = np.empty((2, 4096, 2048), f)
    for c in range(8):
        b, j = c // 4, c % 4
        out[b].reshape(8, 4, 128, 2048)[:, j] = np.asarray(res.results[c]["out"], f).reshape(8, 128, 2048)
    return out
```

```python
import numpy as np
from contextlib import ExitStack
import concourse.bass as bass
import concourse.mybir as mybir
from concourse.bass_utils import run_bass_kernel_spmd

F32 = mybir.dt.float32
BF16 = mybir.dt.bfloat16
AF = mybir.ActivationFunctionType
ALU = mybir.AluOpType
AX = mybir.AxisListType

SAME_ENGINE_SYNC = True
NIT = 20
EPS = 1e-6
KB = 1024
ARENA = 204 * KB

C_U, C_V, C_Q, C_CKV, C_QI, C_KI, C_WI, C_GA, C_GB = 0, 2048, 4096, 6144, 6656, 7680, 7744, 7760, 9808
N_IN = 11856
DFF = 5632


class Prog:
    ENG = ("pe", "act", "dve", "pool", "sp")

    def __init__(self, nc):
        self.nc = nc
        self.ops = []
        self.bufs = {}
        self.res = {}

    def eng(self, name):
        nc = self.nc
        return {"pe": nc.tensor, "act": nc.scalar, "dve": nc.vector,
                "pool": nc.gpsimd, "sp": nc.sync}[name]

    def buf(self, name, lo=None, hi=None):
        inherit = {}
        if lo is not None:
            for n, b in self.bufs.items():
                if b["lo"] is not None and b["lo"] < hi and lo < b["hi"]:
                    b["open"] = False
                    for src in (b["touched"], b["inherit"]):
                        for ch, o in src.items():
                            if inherit.get(ch, -1) < o:
                                inherit[ch] = o
        assert name not in self.bufs or not self.bufs[name]["open"], name
        if name in self.bufs:
            self.bufs[name + "#old%d" % len(self.bufs)] = self.bufs[name]
        self.bufs[name] = dict(lo=lo, hi=hi, open=True, touched={}, inherit=inherit)
        for k in [k for k in self.res if k[0] == name]:
            del self.res[k]

    def _chan(self, o):
        return ("e", o["eng"]) if o["kind"] == "c" else ("d", o["sem_key"])

    def _add(self, kind, eng, fn, reads, writes, sem_key=None):
        idx = len(self.ops)
        o = dict(kind=kind, eng=eng, fn=fn, sem_key=sem_key, signal=False)
        ch = self._chan(o)
        deps = {}

        def add(ch2, op2):
            if op2 is None:
                return
            if deps.get(ch2, -1) < op2:
                deps[ch2] = op2
        for key in list(reads) + list(writes):
            b = self.bufs[key[0]]
            assert b["open"], ("closed buffer", key)
            for ch2, op2 in b["inherit"].items():
                add(ch2, op2)
        for key in reads:
            st = self.res.get(key)
            if st is not None and st["w"] is not None:
                add(self._chan(self.ops[st["w"]]), st["w"])
        for key in writes:
            st = self.res.get(key)
            if st is not None:
                if st["w"] is not None:
                    add(self._chan(self.ops[st["w"]]), st["w"])
                for ch2, op2 in st["r"].items():
                    add(ch2, op2)
        for key in reads:
            st = self.res.setdefault(key, dict(w=None, r={}))
            st["r"][ch] = idx
            self.bufs[key[0]]["touched"][ch] = idx
        for key in writes:
            st = self.res.setdefault(key, dict(w=None, r={}))
            st["w"] = idx
            st["r"] = {}
            self.bufs[key[0]]["touched"][ch] = idx
        o["deps"] = deps
        self.ops.append(o)

    def op(self, eng, fn, reads=(), writes=()):
        self._add("c", eng, fn, reads, writes)

    def dma(self, queue, fn, reads=(), writes=(), sem_key=None):
        self._add("d", queue, fn, reads, writes, sem_key=sem_key if sem_key is not None else tuple(writes)[0])

    def _skip(self, o, D):
        return (D["kind"] == "c" and o["kind"] == "c" and D["eng"] == o["eng"]
                and (o["eng"] == "pe" or not SAME_ENGINE_SYNC))

    def emit(self, stack):
        nc = self.nc
        ops = self.ops
        for o in ops:
            for ch, d in o["deps"].items():
                D = ops[d]
                if D["kind"] == "c" and not self._skip(o, D):
                    D["signal"] = True
        esem = {e: stack.enter_context(nc.semaphore("s_" + e)) for e in self.ENG}
        dsem = {}
        for o in ops:
            if o["kind"] == "d" and o["sem_key"] not in dsem:
                dsem[o["sem_key"]] = stack.enter_context(nc.semaphore("d%d" % len(dsem)))
        ecount = {e: 0 for e in self.ENG}
        dcount = {k: 0 for k in dsem}
        for o in ops:
            if o["kind"] == "c":
                if o["signal"]:
                    ecount[o["eng"]] += 1
                o["cnt"] = ecount[o["eng"]]
            else:
                dcount[o["sem_key"]] += 16
                o["cnt"] = dcount[o["sem_key"]]
        waited = {e: {} for e in self.ENG}
        nw = 0
        for o in ops:
            e = o["eng"]
            E = self.eng(e)
            for ch, d in o["deps"].items():
                D = ops[d]
                if self._skip(o, D):
                    continue
                sem = esem[D["eng"]] if D["kind"] == "c" else dsem[D["sem_key"]]
                val = D["cnt"]
                if waited[e].get(ch, -1) >= val:
                    continue
                E.wait_ge(sem, val)
                waited[e][ch] = val
                nw += 1
            inst = o["fn"](E)
            if o["kind"] == "c":
                if o["signal"]:
                    inst.then_inc(esem[e], 1)
            else:
                inst.then_inc(dsem[o["sem_key"]], 16)
        sp = nc.sync
        for k, s in dsem.items():
            if dcount[k] > 0:
                sp.wait_ge(s, dcount[k])
        for e in ("pe", "act", "dve"):
            if ecount[e] > 0:
                sp.wait_ge(esem[e], ecount[e])
        self.nwaits = nw
        self.nsems = len(dsem) + len(esem)


class Rot:
    def __init__(self, items):
        self.items = list(items)
        self.i = 0

    def next(self):
        v = self.items[self.i % len(self.items)]
        self.i += 1
        return v


def build_program(debug=False):
    nc = bass.Bass("TRN2", target_bir_lowering=False)

    def din(name, shape):
        return nc.dram_tensor(name, shape, F32, kind="ExternalInput").ap()

    x_all = din("x_all", [4096, 2048])
    x_own = din("x_own", [1024, 2048])
    admb_d = din("admb", [128, 512])
    ident_d = din("ident", [128, 128])
    pow2_d = din("pow2", [128, 32])
    g1bc_d = din("g1bc", [128, 2048])
    kvgbc_d = din("kvgbc", [128, 512])
    lngbc_d = din("lngbc", [128, 2048])
    lnbbc_d = din("lnbbc", [128, 2048])
    bsbc_d = din("bsbc", [128, 2048])
    wsT_d = din("wsT", [128, 1024])
    g2bc_d = din("g2bc", [128, 2048])
    gfbc_d = din("gfbc", [128, 2048])
    w_in = din("w_in", [2048, N_IN])
    w_uk = din("w_uk", [512, 2048])
    w_uv = din("w_uv", [512, 2048])
    w_oa = din("w_oa", [2048, 2048])
    w_ob = din("w_ob", [2048, 2048])
    w_out = din("w_out", [2048, 2048])
    w_fg = din("w_ff_gate", [2048, DFF])
    w_fu = din("w_ff_up", [2048, DFF])
    w_fd = din("w_ff_down", [DFF, 2048])
    out_d = nc.dram_tensor("out", [1024, 2048], F32, kind="ExternalOutput").ap()

    win = w_in.rearrange("(kc p) n -> p kc n", p=128)
    wuk = w_uk.rearrange("(kc p) n -> p kc n", p=128)
    wuv = w_uv.rearrange("(kc p) n -> p kc n", p=128)
    woa = w_oa.rearrange("(kc p) n -> p kc n", p=128)
    wob = w_ob.rearrange("(kc p) n -> p kc n", p=128)
    wout = w_out.rearrange("(kc p) n -> p kc n", p=128)
    wfg = w_fg.rearrange("(kc p) n -> p kc n", p=128)
    wfu = w_fu.rearrange("(kc p) n -> p kc n", p=128)
    wfd = w_fd.rearrange("(kc p) n -> p kc n", p=128)

    st = ExitStack()
    arena = st.enter_context(nc.sbuf_tensor("arena", [128, ARENA // 4], F32))
    identb = st.enter_context(nc.sbuf_tensor("identb", [128, 128], BF16))
    ps = [st.enter_context(nc.psum_tensor("ps%d" % i, [128, 512], F32)) for i in range(8)]
    psb = [p[:].bitcast(BF16) for p in ps]
    P = Prog(nc)

    def V(off, shape, dt):
        sz = 4 if dt == F32 else 2
        n = int(np.prod(shape)) * sz
        assert off % 4 == 0 and n % 4 == 0 and off + n <= ARENA, (off, shape)
        ap = arena[:, off // 4:(off + n) // 4]
        if dt != F32:
            ap = ap.bitcast(dt)
        if len(shape) == 2:
            ap = ap.rearrange("p (a b) -> p a b", a=shape[0])
        elif len(shape) == 3:
            ap = ap.rearrange("p (a b c) -> p a b c", a=shape[0], b=shape[1])
        return ap

    def VB(name, off, shape, dt):
        sz = 4 if dt == F32 else 2
        n = int(np.prod(shape)) * sz
        P.buf(name, off, off + n)
        return V(off, shape, dt)

    def mm(out, lhsT, rhs, start, stop, reads, writes):
        P.op("pe", lambda e: e.matmul(out, lhsT=lhsT, rhs=rhs, start=start, stop=stop), reads, writes)

    def tr(out, in_, reads, writes):
        P.op("pe", lambda e: e.transpose(out=out, in_=in_, identity=identb[:]), list(reads) + [("identb",)], writes)

    def act(out, in_, func, reads, writes, **kw):
        P.op("act", lambda e: e.activation(out=out, in_=in_, func=func, **kw), reads, writes)

    def ts(out, in0, s1, s2, op0, op1, reads, writes, accum=None, eng="dve"):
        if op1 is None:
            P.op(eng, lambda e: e.tensor_scalar(out=out, in0=in0, scalar1=s1, scalar2=None, op0=op0), reads, writes)
        elif accum is None:
            P.op(eng, lambda e: e.tensor_scalar(out=out, in0=in0, scalar1=s1, scalar2=s2, op0=op0, op1=op1), reads, writes)
        else:
            P.op(eng, lambda e: e.tensor_scalar(out=out, in0=in0, scalar1=s1, scalar2=s2, op0=op0, op1=op1,
                                                accum_out=accum), reads, writes)

    def stt(out, in0, scalar, in1, op0, op1, reads, writes, eng="dve"):
        P.op(eng, lambda e: e.scalar_tensor_tensor(out=out, in0=in0, scalar=scalar, in1=in1, op0=op0, op1=op1),
             reads, writes)

    def tt(out, in0, in1, op, reads, writes, eng="dve"):
        P.op(eng, lambda e: e.tensor_tensor(out=out, in0=in0, in1=in1, op=op), reads, writes)

    def cp(eng, out, in_, reads, writes):
        if eng == "act":
            P.op("act", lambda e: e.copy(out=out, in_=in_), reads, writes)
        else:
            P.op(eng, lambda e: e.tensor_copy(out=out, in_=in_), reads, writes)

    def ld(dst, src, key, queue="sp", reads=()):
        P.dma(queue, lambda e: e.dma_start(out=dst, in_=src), reads, [key])

    def ldw(dst, src, key):
        P.dma("pool", lambda e: e.dma_start(out=dst, in_=src), (), [key])

    evac_rot = Rot(["act", "dve"])
    dbg_names = []

    def dbg(name, ap, keys, shape, dt):
        if not debug:
            return
        d = nc.dram_tensor("dbg_" + name, [128] + list(shape), dt, kind="ExternalOutput").ap()
        P.buf("dbg_" + name)
        dbg_names.append("dbg_" + name)
        P.dma("sp", lambda e: e.dma_start(out=d, in_=ap), keys, [("dbg_" + name,)])

    for i in range(8):
        P.buf("ps%d" % i)
    P.buf("identb")
    P.buf("out")
    PSK = [("ps%d" % i,) for i in range(8)]

    SM = 196 * KB
    sm_off = [SM]

    def Vs(n, dt=F32):
        off = sm_off[0]
        sz = n * (4 if dt == F32 else 2)
        sz = (sz + 3) // 4 * 4
        sm_off[0] += sz
        assert sm_off[0] <= ARENA
        ap = arena[:, off // 4:(off + sz) // 4]
        if dt != F32:
            ap = ap.bitcast(dt)
        return ap

    P.buf("sm", SM, ARENA)
    NACC = 32 + 32 + 8 + 8 + 8 + 32 + 8 + NIT * 8
    acc = Vs(NACC)
    o = 0
    ssK = acc[:, o:o + 32]; o += 32
    ssc = acc[:, o:o + 32]; o += 32
    ssQ = acc[:, o:o + 8]; o += 8
    ss2 = acc[:, o:o + 8]; o += 8
    ssF = acc[:, o:o + 8]; o += 8
    vsum = acc[:, o:o + 32]; o += 32
    vsq = acc[:, o:o + 8]; o += 8
    cnt = acc[:, o:o + NIT * 8]; o += NIT * 8
    rsK = Vs(32); rsc = Vs(32); rsQ = Vs(8); rs2 = Vs(8); rsF = Vs(8)
    vmean = Vs(8); vvar = Vs(8); vrstd = Vs(8); vtmp = Vs(8)
    wsc = Vs(128)
    rmin = Vs(8); rmax = Vs(8); w0 = Vs(8); lo0 = Vs(8); mid = Vs(8); thr = Vs(8); tmpc = Vs(8)
    hw = Vs(8 * 32)
    pow2 = Vs(32)
    rcp = Vs(4)
    wsT = Vs(1024, BF16)
    svt = Vs(512).rearrange("p (a b) -> p a b", a=4)
    wsT3 = wsT.rearrange("p (a b) -> p a b", a=8)
    hw3 = hw.rearrange("p (a b) -> p a b", a=8)
    wsc3 = wsc.rearrange("p (a b) -> p a b", a=8)

    P.op("dve", lambda e: e.memset(acc, 0.0), (), [("sm", "acc")])
    ld(pow2, pow2_d, ("sm", "pow2"))
    ldw(identb[:], ident_d, ("identb",))

    def rstd_batch(ss_ap, rs_ap, inv_n, skeys, rkey):
        ts(rs_ap, ss_ap, inv_n, EPS, ALU.mult, ALU.add, skeys, [rkey])
        act(rs_ap, rs_ap, AF.Sqrt, [rkey], [rkey])
        P.op("dve", lambda e: e.reciprocal(out=rs_ap, in_=rs_ap), [rkey], [rkey])

    cT = VB("cT", 0, [4, 4096], BF16)
    xg = [VB("xg%d" % s, 32 * KB + s * 16 * KB, [16, 512], BF16) for s in range(2)]
    Wk = VB("Wk", 64 * KB, [16, 640], BF16)
    kvg = VB("kvg", 84 * KB, [512], F32)
    cn = [VB("cn%d" % s, 86 * KB + s * KB, [512], BF16) for s in range(2)]
    xn = [VB("xn%d" % s, 88 * KB + s * 4 * KB, [2048], BF16) for s in range(2)]
    kx = VB("kx", 96 * KB, [4096], BF16)
    qx = VB("qx", 104 * KB, [8, 1024], BF16)
    sqj = VB("sqj", 120 * KB, [2048], F32)
    xt = [VB("xt%d" % s, 156 * KB + s * 8 * KB, [2048], F32) for s in range(4)]
    g1 = VB("g1", 188 * KB, [2048], F32)

    ld(g1, g1bc_d, ("g1",))
    ld(kvg, kvgbc_d, ("kvg",))
    ldw(Wk[:, :, 0:512], win[:, :, C_CKV:C_CKV + 512], ("Wk", 0))
    ldw(Wk[:, :, 512:576], win[:, :, C_KI:C_KI + 64], ("Wk", 1))
    ldw(Wk[:, :, 576:640], win[:, :, C_KI:C_KI + 64], ("Wk", 2))
    WKK = [("Wk", 0), ("Wk", 1), ("Wk", 2)]

    front_evac = [None, None]

    def front_pre(src_d, row0, ss_ap, rs_ap, sbase):
        for tt_ in range(4):
            ld(xt[tt_], src_d[row0 + tt_ * 128: row0 + (tt_ + 1) * 128, :], ("xt%d" % tt_,))
        rkeys = []
        for hf in range(2):
            for tt_ in (2 * hf, 2 * hf + 1):
                c = sbase + tt_
                act(sqj, xt[tt_], AF.Square, [("xt%d" % tt_,), ("sm", "acc")], [("sqj",), ("sm", "ss", id(ss_ap), c)],
                    accum_out=ss_ap[:, c:c + 1])
            rkey = ("sm", "rs", id(rs_ap), sbase + 2 * hf)
            c0 = sbase + 2 * hf
            rstd_batch(ss_ap[:, c0:c0 + 2], rs_ap[:, c0:c0 + 2], 1.0 / 2048,
                       [("sm", "ss", id(ss_ap), c0 + k) for k in range(2)], rkey)
            rkeys.append(rkey)
        return rkeys

    def front_stt(tt_, rs_ap, sbase, rkey, gbc, gbckey):
        c = sbase + tt_
        s = tt_ % 2
        stt(xn[s], xt[tt_], rs_ap[:, c:c + 1], gbc, ALU.mult, ALU.mult,
            [("xt%d" % tt_,), rkey[tt_ // 2], gbckey], [("xn%d" % s,)])

    def front_te(tt_, dst_fn, dkey_fn):
        s = tt_ % 2
        for kc in range(16):
            tr(psb[kc // 8][:, (kc % 8) * 128:(kc % 8 + 1) * 128], xn[s][:, kc * 128:(kc + 1) * 128],
               [("xn%d" % s,)], [PSK[kc // 8]])
        for half in range(2):
            cp(front_evac[half] if front_evac[0] else evac_rot.next(), dst_fn(tt_, half),
               psb[half][:, 0:1024].rearrange("p (a b) -> p a b", a=8), [PSK[half]], [dkey_fn(tt_)])

    def front_tile(tt_, rs_ap, sbase, rkey, dst_fn, dkey_fn, gbc, gbckey):
        front_stt(tt_, rs_ap, sbase, rkey, gbc, gbckey)
        front_te(tt_, dst_fn, dkey_fn)

    def front(src_d, row0, ss_ap, rs_ap, sbase, gkey_fn, dst_fn, dkey_fn, gbc, gbckey):
        rkey = front_pre(src_d, row0, ss_ap, rs_ap, sbase)
        front_stt(0, rs_ap, sbase, rkey, gbc, gbckey)
        for tt_ in range(4):
            if tt_ + 1 < 4:
                front_stt(tt_ + 1, rs_ap, sbase, rkey, gbc, gbckey)
            front_te(tt_, dst_fn, dkey_fn)

    def back_ckv(g, tt_):
        gs = g % 2
        T = 4 * g + tt_
        b = 2 + tt_
        for kc in range(16):
            mm(ps[b][:, :], xg[gs][:, kc, tt_ * 128:(tt_ + 1) * 128], Wk[:, kc, 0:512], kc == 0, kc == 15,
               [("xg%d" % gs, tt_)] + WKK, [PSK[b]])
        act(sqj[:, 0:512], ps[b][:, :], AF.Square, [PSK[b], ("sm", "acc")], [("sqj",), ("sm", "ssc", T)],
            accum_out=ssc[:, T:T + 1])

    def c_rstd(g):
        rkey = ("sm", "rsc", g)
        rstd_batch(ssc[:, 4 * g:4 * g + 4], rsc[:, 4 * g:4 * g + 4], 1.0 / 512,
                   [("sm", "ssc", 4 * g + k) for k in range(4)], rkey)

    def post_pre(g):
        gs = g % 2
        rkey = ("sm", "rsc", g)
        for kc in range(16):
            mm(ps[7][:, :], Wk[:, kc, 512:640], xg[gs][:, kc, :], kc == 0, kc == 15,
               [("xg%d" % gs, t4) for t4 in range(4)] + WKK, [PSK[7]])
        cp("dve", kx[:, g * 512:(g + 1) * 512], ps[7][:, :], [PSK[7]], [("kx", g)])
        return rkey

    def post_tile(g, tt_, rkey):
        T = 4 * g + tt_
        b = 2 + tt_
        s = T % 2
        stt(cn[s], ps[b][:, :], rsc[:, T:T + 1], kvg, ALU.mult, ALU.mult,
            [PSK[b], rkey, ("kvg",)], [("cn%d" % s,)])
        for cc in range(4):
            tr(psb[6][:, cc * 128:(cc + 1) * 128], cn[s][:, cc * 128:(cc + 1) * 128], [("cn%d" % s,)], [PSK[6]])
        cp("dve", cT[:, :, T * 128:(T + 1) * 128],
           psb[6][:, 0:512].rearrange("p (a b) -> p a b", a=4), [PSK[6]], [("cT", g)])

    def kdst(gs):
        return (lambda tt_, half: xg[gs][:, half * 8:(half + 1) * 8, tt_ * 128:(tt_ + 1) * 128],
                lambda tt_: ("xg%d" % gs, tt_))

    def k_stats_emit(g):
        for tt_ in range(4):
            ld(xt[tt_], x_all[(4 * g + tt_) * 128:(4 * g + tt_ + 1) * 128, :], ("xt%d" % tt_,))
        for hf in range(2):
            for tt_ in (2 * hf, 2 * hf + 1):
                c_ = 4 * g + tt_
                act(sqj, xt[tt_], AF.Square, [("xt%d" % tt_,), ("sm", "acc")], [("sqj",), ("sm", "ss", id(ssK), c_)],
                    accum_out=ssK[:, c_:c_ + 1])
            c0 = 4 * g + 2 * hf
            act(rsK[:, c0:c0 + 2], ssK[:, c0:c0 + 2], AF.Sqrt, [("sm", "ss", id(ssK), c0 + k) for k in range(2)],
                [("sm", "rs", id(rsK), c0)], scale=1.0 / 2048, bias=EPS)

    def k_recip(g):
        keys = []
        for hf in range(2):
            c0 = 4 * g + 2 * hf
            key = ("sm", "rs", id(rsK), c0)
            P.op("dve", lambda e, c0=c0: e.reciprocal(out=rsK[:, c0:c0 + 2], in_=rsK[:, c0:c0 + 2]), [key], [key])
            keys.append(key)
        return keys

    front_evac[0], front_evac[1] = "act", "act"
    k_stats_emit(0)
    rk = k_recip(0)
    d1, k1 = kdst(0)
    for tt_ in range(4):
        front_tile(tt_, rsK, 0, rk, d1, k1, g1, ("g1",))
    k_stats_emit(1)
    for g in range(9):
        prk = post_pre(g - 1) if g >= 1 else None

        def Pt(tt_):
            if g >= 1:
                post_tile(g - 1, tt_, prk)

        def Ct(tt_):
            if g < 8:
                back_ckv(g, tt_)

        def St(tt_):
            if g + 1 < 8:
                front_stt(tt_, rsK, 4 * (g + 1), rk, g1, ("g1",))

        def Te(tt_):
            if g + 1 < 8:
                front_te(tt_, d1, k1)
        Pt(0); Ct(0); Pt(1)
        if g + 1 < 8:
            rk = k_recip(g + 1)
            d1, k1 = kdst((g + 1) % 2)
        St(0); Ct(1); Te(0); Pt(2); St(1); Ct(2); Te(1); Pt(3); St(2); Ct(3); Te(2); St(3); Te(3)
        if g < 8:
            c_rstd(g)
        if g + 2 < 8:
            k_stats_emit(g + 2)
    front_evac[0], front_evac[1] = None, None

    dbg("cT", cT, [("cT", g) for g in range(8)], [4, 4096], BF16)
    dbg("kx", kx, [("kx", g) for g in range(8)], [4096], BF16)
    xnT = VB("xnT", 32 * KB, [16, 1024], BF16)
    XNT = [("xnT", T) for T in range(8)]
    for g in range(2):
        front(x_own, g * 512, ssQ, rsQ, 4 * g, None,
              lambda tt_, half, g=g: xnT[:, half * 8:(half + 1) * 8, (4 * g + tt_) * 128:(4 * g + tt_ + 1) * 128],
              lambda tt_, g=g: ("xnT", 4 * g + tt_), g1, ("g1",))

    Wp = [VB("Wp%d" % s, 64 * KB + s * 16 * KB, [16, 512], BF16) for s in range(2)]
    Wwi = VB("Wwi", 156 * KB, [16, 16], BF16)
    prb = Rot([0, 1, 2, 3])
    for pn in range(2):
        ldw(Wp[pn], win[:, :, C_QI + pn * 512:C_QI + (pn + 1) * 512], ("Wp%d" % pn,))
        for c in range(4):
            hp = pn * 4 + c
            for th in range(2):
                b = prb.next()
                for kc in range(16):
                    mm(ps[b][:, :], Wp[pn][:, kc, c * 128:(c + 1) * 128], xnT[:, kc, th * 512:(th + 1) * 512],
                       kc == 0, kc == 15, [("Wp%d" % pn,)] + XNT, [PSK[b]])
                cp(evac_rot.next(), qx[:, hp, th * 512:(th + 1) * 512], ps[b][:, :], [PSK[b]], [("qx", hp)])
    ldw(Wwi, win[:, :, C_WI:C_WI + 16], ("Wwi",))
    for T in range(8):
        b = prb.next()
        for kc in range(16):
            mm(ps[b][:, 0:16], xnT[:, kc, T * 128:(T + 1) * 128], Wwi[:, kc, :], kc == 0, kc == 15,
               [("Wwi",)] + XNT, [PSK[b]])
        ts(wsc3[:, T, :], ps[b][:, 0:16], 1.0 / 32.0, None, ALU.mult, None, [PSK[b]], [("sm", "wsc", T)])

    dbg("xnT", xnT, XNT, [16, 1024], BF16)
    dbg("qx", qx, [("qx", hp) for hp in range(8)], [8, 1024], BF16)
    dbg("wsc", wsc, [("sm", "wsc", T) for T in range(8)], [128], F32)
    score = [VB("score%d" % s, 64 * KB + s * 16 * KB, [4096], F32) for s in range(2)]
    MT = VB("MT", 120 * KB, [144, 128], BF16)
    Rs = [VB("R%d" % s, 156 * KB + s * KB, [512], BF16) for s in range(4)]
    admb = VB("admb", 160 * KB, [512], F32)
    Dm = [VB("Dm%d" % s, 162 * KB + s * 4 * KB, [16, 128], BF16) for s in range(2)]
    Mtm = VB("Mtm", 170 * KB, [4096], BF16)
    absw = [VB("absw%d" % s, 178 * KB + s * 64, [16], F32) for s in range(2)]
    sgnh = [VB("sgnh%d" % s, 178 * KB + 128 + s * 64, [16], F32) for s in range(2)]
    ld(admb, admb_d, ("admb",))
    kxB = VB("kxB", 180 * KB, [4096], BF16)
    KXA = [("kx", g) for g in range(8)]
    P.op("dve", lambda e: e.memset(kxB[0:64, :], 0.0), (), [("kxB", 0)])
    cp("dve", kxB[64:128, :], kx[64:128, :], KXA, [("kxB", 1)])
    P.op("dve", lambda e: e.memset(kx[64:128, :], 0.0), KXA, KXA)
    lb = Rot([0, 1, 2, 3])
    rr = Rot([0, 1, 2, 3])
    sbr = Rot([4, 5])
    tb = Rot([6, 7])

    def scores(i):
        s2 = i % 2
        sc = score[s2]
        ts(sgnh[s2], wsc3[:, i, :], 0.0, 0.5, ALU.is_gt, ALU.subtract, [("sm", "wsc", i)], [("sgnh%d" % s2,)])
        stt(absw[s2], wsc3[:, i, :], 4.0, sgnh[s2], ALU.mult, ALU.mult, [("sm", "wsc", i), ("sgnh%d" % s2,)], [("absw%d" % s2,)])
        tt(Dm[s2], identb[:].unsqueeze(1).to_broadcast([128, 16, 128]),
           sgnh[s2].unsqueeze(2).to_broadcast([128, 16, 128]), ALU.mult,
           [("identb",), ("sgnh%d" % s2,)], [("Dm%d" % s2,)])
        items = [(sg, h) for sg in range(i + 1) for h in range(16)]
        lbank = {}

        def emitL(n):
            sg, h = items[n]
            hp, base = h // 2, (h % 2) * 64
            bnk = lb.next()
            lbank[n] = bnk
            kk_ = kx if base == 0 else kxB
            mm(ps[bnk][:, :], qx[:, hp, i * 128:(i + 1) * 128], kk_[:, sg * 512:(sg + 1) * 512],
               True, True, [("qx", hp), ("kx", sg), ("kxB", 0), ("kxB", 1)], [PSK[bnk]])
        LA = 2
        for n in range(min(LA, len(items))):
            emitL(n)
        sb = None
        for n in range(len(items)):
            sg, h = items[n]
            if n + LA < len(items):
                emitL(n + LA)
            bnk = lbank.pop(n)
            r = rr.next()
            act(Rs[r], ps[bnk][:, :], AF.Relu, [PSK[bnk], ("absw%d" % s2,)], [("R%d" % r,)], scale=absw[s2][:, h:h + 1])
            if h == 0:
                sb = sbr.next()
            mm(ps[sb][:, :], Dm[s2][:, h, :], Rs[r], h == 0, h == 15, [("Dm%d" % s2,), ("R%d" % r,)], [PSK[sb]])
            if h == 15:
                cp("act", sc[:, sg * 512:(sg + 1) * 512], ps[sb][:, :], [PSK[sb]], [("score%d" % s2, sg)])

    def bisect(i):
        s2 = i % 2
        sc = score[s2]
        nk = 512 * (i + 1)
        SC = [("score%d" % s2, sg) for sg in range(i + 1)]
        bk = ("sm", "bis", i)
        P.op("dve", lambda e: e.tensor_reduce(out=rmin[:, i:i + 1], in_=sc[:, 0:nk], axis=AX.X, op=ALU.min), SC, [bk])
        tt(sc[:, i * 512:(i + 1) * 512], sc[:, i * 512:(i + 1) * 512], admb, ALU.add,
           [("score%d" % s2, i), ("admb",)], [("score%d" % s2, i)])
        P.op("dve", lambda e: e.reduce_max(out=rmax[:, i:i + 1], in_=sc[:, 0:nk], axis=AX.X), SC + [bk], [bk])
        ts(lo0[:, i:i + 1], rmin[:, i:i + 1], -1.0, None, ALU.add, None, [bk], [bk])
        stt(w0[:, i:i + 1], rmax[:, i:i + 1], 1.0, rmin[:, i:i + 1], ALU.add, ALU.subtract, [bk], [bk])
        ts(hw3[:, i, :], pow2, w0[:, i:i + 1], None, ALU.mult, None, [bk, ("sm", "pow2")], [bk])
        tt(mid[:, i:i + 1], lo0[:, i:i + 1], hw3[:, i, 0:1], ALU.add, [bk], [bk])
        for k in range(NIT):
            cc = i * NIT + k
            ts(Mtm[:, 0:nk], sc[:, 0:nk], mid[:, i:i + 1], 0.0, ALU.is_gt, ALU.add,
               SC + [bk, ("sm", "acc")], [("Mtm",), bk], accum=cnt[:, cc:cc + 1])
            ts(tmpc[:, i:i + 1], cnt[:, cc:cc + 1], 256.0, 0.5, ALU.is_ge, ALU.subtract, [bk], [bk])
            stt(mid[:, i:i + 1], tmpc[:, i:i + 1], hw3[:, i, k:k + 1], mid[:, i:i + 1], ALU.mult, ALU.add, [bk], [bk])
        tt(thr[:, i:i + 1], mid[:, i:i + 1], hw3[:, i, NIT:NIT + 1], ALU.subtract, [bk], [bk])
        ts(Mtm[:, 0:nk], sc[:, 0:nk], thr[:, i:i + 1], None, ALU.is_gt, None, SC + [bk], [("Mtm",)])
        off = 2 * i * (i + 1)
        nkt = 4 * (i + 1)
        for k0 in range(0, nkt, 8):
            n = min(8, nkt - k0)
            bnk = tb.next()
            for kk in range(n):
                kt = k0 + kk
                tr(psb[bnk][:, kk * 128:(kk + 1) * 128], Mtm[:, kt * 128:(kt + 1) * 128], [("Mtm",)], [PSK[bnk]])
            cp("act", MT[:, off + k0:off + k0 + n, :],
               psb[bnk][:, 0:n * 128].rearrange("p (a b) -> p a b", a=n), [PSK[bnk]], [("MT", i)])

    scores(0)
    for i in range(8):
        if i + 1 < 8:
            scores(i + 1)
        bisect(i)

    dbg("thr", thr, [("sm", "bis", i) for i in range(8)], [8], F32)
    dbg("rmin", rmin, [("sm", "bis", i) for i in range(8)], [8], F32)
    dbg("rmax", rmax, [("sm", "bis", i) for i in range(8)], [8], F32)
    dbg("cnt", cnt, [("sm", "bis", i) for i in range(8)], [NIT * 8], F32)
    dbg("score7", score[1], [("score1", sg) for sg in range(8)], [4096], F32)
    dbg("MT", MT, [("MT", i) for i in range(8)], [144, 128], BF16)
    ybT = VB("ybT", 64 * KB, [16, 1024], BF16)
    KT = VB("KT", 96 * KB, [2, 4096], BF16)
    Wq = VB("Wq", 112 * KB, [16, 256], BF16)
    Vaug = VB("Vaug", 156 * KB, [32, 2, 129], BF16)
    o_ = 156 * KB + 16512
    wuk_s = VB("wuk_s", o_, [4, 256], BF16); o_ += 2 * KB
    wuv_s = VB("wuv_s", o_, [4, 256], BF16); o_ += 2 * KB
    qT = VB("qT", o_, [2, 1024], BF16); o_ += 4 * KB
    Es = []
    for s in range(3):
        Es.append(VB("E%d" % s, o_, [512], BF16)); o_ += KB
    PTs = []
    for s in range(3):
        PTs.append(VB("PT%d" % s, o_, [512], BF16)); o_ += KB
    Ons = []
    for s in range(2):
        Ons.append(VB("On%d" % s, o_, [128], BF16)); o_ += 256
    assert o_ <= 196 * KB
    P.op("dve", lambda e: e.memset(Vaug[:, :, :, 128:129], 1.0), (), [("Vaug", "ones")])
    CT = [("cT", g) for g in range(8)]
    SCALE = float(128 ** -0.5)
    pjb = Rot([6, 7])
    sbk = Rot([0, 1, 2])
    esr = Rot([0, 1, 2])
    onr = Rot([0, 1])
    blk_ctr = [0]
    for hg in range(8):
        ldw(wuk_s, wuk[:, :, hg * 256:(hg + 1) * 256], ("wuk_s",))
        ldw(wuv_s, wuv[:, :, hg * 256:(hg + 1) * 256], ("wuv_s",))
        ldw(Wq, win[:, :, C_Q + hg * 256:C_Q + (hg + 1) * 256], ("Wq",))
        for hh in range(2):
            for th in range(2):
                b = pjb.next()
                for kc in range(16):
                    mm(ps[b][:, :], Wq[:, kc, hh * 128:(hh + 1) * 128], xnT[:, kc, th * 512:(th + 1) * 512],
                       kc == 0, kc == 15, [("Wq",)] + XNT, [PSK[b]])
                cp(evac_rot.next(), qT[:, hh, th * 512:(th + 1) * 512], ps[b][:, :], [PSK[b]], [("qT", hh)])
        for hh in range(2):
            for sg in range(8):
                b = pjb.next()
                for cc in range(4):
                    mm(ps[b][:, :], wuk_s[:, cc, hh * 128:(hh + 1) * 128], cT[:, cc, sg * 512:(sg + 1) * 512],
                       cc == 0, cc == 3, [("wuk_s",), ("cT", sg)], [PSK[b]])
                cp(evac_rot.next(), KT[:, hh, sg * 512:(sg + 1) * 512], ps[b][:, :], [PSK[b]], [("KT", hh, sg)])
        for st2 in range(16):
            b = pjb.next()
            for j in range(2):
                stile = 2 * st2 + j
                for cc in range(4):
                    mm(ps[b][:, j * 256:(j + 1) * 256], cT[:, cc, stile * 128:(stile + 1) * 128], wuv_s[:, cc, :],
                       cc == 0, cc == 3, [("wuv_s",), ("cT", stile // 4)], [PSK[b]])
            cp(evac_rot.next(), Vaug[:, 2 * st2:2 * st2 + 2, :, 0:128],
               ps[b][:, :].rearrange("p (a b c) -> p a b c", a=2, b=2), [PSK[b]], [("Vaug", st2 // 2)])
        if hg < 0:
            dbg("KT%d" % hg, KT, [("KT", hh, sg) for hh in range(2) for sg in range(8)], [2, 4096], BF16)
            dbg("Vaug%d" % hg, Vaug, [("Vaug", q) for q in range(8)] + [("Vaug", "ones")], [32, 2, 129], BF16)
            dbg("qT%d" % hg, qT, [("qT", 0), ("qT", 1)], [2, 1024], BF16)
            dbg("wuv%d" % hg, wuv_s, [("wuv_s",)], [4, 256], BF16)
        groups = [(hh, i, sgi) for hh in range(2) for i in range(8) for sgi in range(i + 1)]
        sbank = {}

        def emitS(n):
            hh, i, sgi = groups[n]
            b = sbk.next()
            sbank[n] = b
            for k4 in range(4):
                kt = 4 * sgi + k4
                mm(ps[b][:, k4 * 128:(k4 + 1) * 128], KT[:, hh, kt * 128:(kt + 1) * 128], qT[:, hh, i * 128:(i + 1) * 128],
                   True, True, [("KT", hh, sgi), ("qT", hh)], [PSK[b]])

        deferred = []

        def run_deferred(upto):
            deferred.sort(key=lambda t: (t[0], t[1]))
            while deferred and deferred[0][0] <= upto:
                deferred.pop(0)[2]()

        seq = [0]

        def schedule_final(n, hh, i, ob):
            s = onr.next()

            def f_dve():
                P.op("dve", lambda e: e.reciprocal(out=rcp[:, 0:1], in_=ps[ob][:, 128:129]), [PSK[ob]], [("sm", "rcp")])
                ts(Ons[s], ps[ob][:, 0:128], rcp[:, 0:1], None, ALU.mult, None, [PSK[ob], ("sm", "rcp")], [("On%d" % s,)])
            tbank = [None]

            def f_tr():
                tbank[0] = pjb.next()
                tr(psb[tbank[0]][:, 0:128], Ons[s], [("On%d" % s,)], [PSK[tbank[0]]])

            def f_ev():
                cp("act", ybT[:, 2 * hg + hh, i * 128:(i + 1) * 128], psb[tbank[0]][:, 0:128], [PSK[tbank[0]]],
                   [("ybT", 2 * hg + hh, i)])
            for due, fn in ((n + 2, f_dve), (n + 3, f_tr), (n + 5, f_ev)):
                seq[0] += 1
                deferred.append((due, seq[0], fn))

        emitS(0)
        emitS(1)
        for n in range(len(groups)):
            hh, i, sgi = groups[n]
            if n + 2 < len(groups):
                emitS(n + 2)
            b = sbank.pop(n)
            es = esr.next()
            act(Es[es], ps[b][:, :], AF.Exp, [PSK[b]], [("E%d" % es,)], scale=SCALE)
            off = 2 * i * (i + 1)
            tt(PTs[es], Es[es], MT[:, off + 4 * sgi:off + 4 * sgi + 4, :].rearrange("p a b -> p (a b)"), ALU.mult,
               [("E%d" % es,), ("MT", i)], [("PT%d" % es,)])
            if sgi == 0:
                blk_ctr[0] += 1
            ob = 3 + (blk_ctr[0] % 3)
            for k4 in range(4):
                kt = 4 * sgi + k4
                mm(ps[ob][:, 0:129], PTs[es][:, k4 * 128:(k4 + 1) * 128], Vaug[:, kt, hh, :],
                   (sgi == 0 and k4 == 0), (sgi == i and k4 == 3),
                   [("PT%d" % es,), ("Vaug", kt // 4), ("Vaug", "ones")], [PSK[ob]])
            run_deferred(n)
            if sgi == i:
                schedule_final(n, hh, i, ob)
        run_deferred(10 ** 9)

    dbg("ybT", ybT, [("ybT", h, i) for h in range(16) for i in range(8)], [16, 1024], BF16)
    yaT = VB("yaT", 96 * KB, [16, 1024], BF16)
    vg = VB("vg", 128 * KB, [8, 2048], BF16)
    lng = VB("lng", 160 * KB, [2048], F32)
    lnb = VB("lnb", 168 * KB, [2048], F32)
    vln = [VB("vln0", 176 * KB, [2048], BF16)]
    vtf = VB("vtf", 180 * KB, [2048], F32)
    bsb = VB("bsb", 188 * KB, [16, 128], F32)
    WA = [VB("WA%d" % s, s * 16 * KB, [16, 512], BF16) for s in range(2)]
    ld(lng, lngbc_d, ("lng",))
    ld(lnb, lnbbc_d, ("lnb",))
    ld(bsb, bsbc_d.rearrange("p (a b) -> p a b", a=16), ("bsb",))
    ldw(wsT3, wsT_d.rearrange("p (a b) -> p a b", a=8), ("sm", "wsT"))
    P.op("dve", lambda e: e.memset(wsT3[64:128, :, 0:64], 0.0), [("sm", "wsT")], [("sm", "wsT")])
    war = Rot([0, 1])
    ab = Rot([0, 1, 2, 3])
    mb = Rot([4, 5, 6, 7])
    for pn in range(4):
        s = war.next()
        ldw(WA[s], win[:, :, C_V + pn * 512:C_V + (pn + 1) * 512], ("WA%d" % s,))
        for T in range(8):
            b = ab.next()
            for kc in range(16):
                mm(ps[b][:, :], xnT[:, kc, T * 128:(T + 1) * 128], WA[s][:, kc, :], kc == 0, kc == 15,
                   [("WA%d" % s,)] + XNT, [PSK[b]])
            cidx = T * 4 + pn
            act(vg[:, T, pn * 512:(pn + 1) * 512], ps[b][:, :], AF.Gelu_apprx_tanh, [PSK[b], ("sm", "acc")],
                [("vg", T, pn), ("sm", "vsum", cidx)], accum_out=vsum[:, cidx:cidx + 1])
    for T in range(8):
        VG = [("vg", T, pn) for pn in range(4)]
        act(vtf, vg[:, T, :], AF.Square, VG + [("sm", "acc")], [("vtf",), ("sm", "vsq", T)], accum_out=vsq[:, T:T + 1])
    sk = ("sm", "vstat")
    P.op("dve", lambda e: e.reduce_sum(out=vmean[:, 0:8], in_=vsum[:, 0:32].rearrange("p (a b) -> p a b", b=4), axis=AX.X),
         [("sm", "vsum", c) for c in range(32)], [sk])
    ts(vmean[:, 0:8], vmean[:, 0:8], 1.0 / 2048, None, ALU.mult, None, [sk], [sk])
    tt(vtmp[:, 0:8], vmean[:, 0:8], vmean[:, 0:8], ALU.mult, [sk], [sk])
    stt(vvar[:, 0:8], vsq[:, 0:8], 1.0 / 2048, vtmp[:, 0:8], ALU.mult, ALU.subtract,
        [sk] + [("sm", "vsq", T) for T in range(8)], [sk])
    ts(vrstd[:, 0:8], vvar[:, 0:8], EPS, None, ALU.add, None, [sk], [sk])
    act(vrstd[:, 0:8], vrstd[:, 0:8], AF.Sqrt, [sk], [sk])
    P.op("dve", lambda e: e.reciprocal(out=vrstd[:, 0:8], in_=vrstd[:, 0:8]), [sk], [sk])
    stt(vtmp[:, 0:8], vmean[:, 0:8], -1.0, vrstd[:, 0:8], ALU.mult, ALU.mult, [sk], [sk])
    for pn in range(4):
        s = war.next()
        ldw(WA[s], win[:, :, C_U + pn * 512:C_U + (pn + 1) * 512], ("WA%d" % s,))
        for c in range(4):
            ch = pn * 4 + c
            for th in range(2):
                b = ab.next()
                for kc in range(16):
                    mm(ps[b][:, :], WA[s][:, kc, c * 128:(c + 1) * 128], xnT[:, kc, th * 512:(th + 1) * 512],
                       kc == 0, kc == 15, [("WA%d" % s,)] + XNT, [PSK[b]])
                act(yaT[:, ch, th * 512:(th + 1) * 512], ps[b][:, :], AF.Gelu_apprx_tanh, [PSK[b]], [("yaT", ch, th)])
    for T in range(8):
        VG = [("vg", T, pn) for pn in range(4)]
        act(vtf, vg[:, T, :], AF.Identity, VG + [sk], [("vtf",)], scale=vrstd[:, T:T + 1], bias=vtmp[:, T:T + 1])
        tt(vtf, vtf, lng, ALU.mult, [("vtf",), ("lng",)], [("vtf",)])
        tt(vln[0], vtf, lnb, ALU.add, [("vtf",), ("lnb",)], [("vln0",)])
        for cb in range(4):
            b = mb.next()
            for c4 in range(4):
                ch = cb * 4 + c4
                mm(ps[b][:, c4 * 128:(c4 + 1) * 128], vln[0][:, ch * 128:(ch + 1) * 128], wsT3[:, ch // 2, :],
                   True, True, [("vln0",), ("sm", "wsT")], [PSK[b]])
            ysl = yaT[:, cb * 4:(cb + 1) * 4, T * 128:(T + 1) * 128]
            YK = [("yaT", cb * 4 + c4, T // 4) for c4 in range(4)]
            tt(svt, ps[b][:, :].rearrange("p (a b) -> p a b", a=4), bsb[:, cb * 4:(cb + 1) * 4, :], ALU.add,
               [PSK[b], ("bsb",)], [("sm", "svt")])
            tt(ysl, svt, ysl, ALU.mult, [("sm", "svt")] + YK, YK)

    dbg("yaT", yaT, [("yaT", ch, th) for ch in range(16) for th in range(2)], [16, 1024], BF16)
    mixT = VB("mixT", 0, [16, 1024], BF16)
    WM = {}
    o_ = 128 * KB
    for nm in ("oa", "ob", "ga", "gb"):
        for s in range(2):
            WM[(nm, s)] = VB("WM%s%d" % (nm, s), o_, [16, 256], BF16)
            o_ += 8 * KB
    gaf = VB("gaf", 192 * KB, [512], F32)
    gbf = VB("gbf", 194 * KB, [512], F32)
    YA = [("yaT", ch, th) for ch in range(16) for th in range(2)]
    YB = [("ybT", h, i) for h in range(16) for i in range(8)]
    xb_ = Rot([0, 1, 2, 3, 4, 5, 6, 7])
    for pn in range(8):
        s = pn % 2
        ldw(WM[("oa", s)], woa[:, :, pn * 256:(pn + 1) * 256], ("WMoa%d" % s,))
        ldw(WM[("ob", s)], wob[:, :, pn * 256:(pn + 1) * 256], ("WMob%d" % s,))
        ldw(WM[("ga", s)], win[:, :, C_GA + pn * 256:C_GA + (pn + 1) * 256], ("WMga%d" % s,))
        ldw(WM[("gb", s)], win[:, :, C_GB + pn * 256:C_GB + (pn + 1) * 256], ("WMgb%d" % s,))
        for c in range(2):
            n = pn * 2 + c
            for th in range(2):
                bA, bB, bGA, bGB = xb_.next(), xb_.next(), xb_.next(), xb_.next()
                tsl = slice(th * 512, (th + 1) * 512)
                for kc in range(16):
                    mm(ps[bGA][:, :], WM[("ga", s)][:, kc, c * 128:(c + 1) * 128], xnT[:, kc, tsl], kc == 0, kc == 15,
                       [("WMga%d" % s,)] + XNT, [PSK[bGA]])
                act(gaf, ps[bGA][:, :], AF.Sigmoid, [PSK[bGA]], [("gaf",)])
                for kc in range(16):
                    mm(ps[bGB][:, :], WM[("gb", s)][:, kc, c * 128:(c + 1) * 128], xnT[:, kc, tsl], kc == 0, kc == 15,
                       [("WMgb%d" % s,)] + XNT, [PSK[bGB]])
                act(gbf, ps[bGB][:, :], AF.Sigmoid, [PSK[bGB]], [("gbf",)])
                for kc in range(16):
                    mm(ps[bB][:, :], WM[("ob", s)][:, kc, c * 128:(c + 1) * 128], ybT[:, kc, tsl], kc == 0, kc == 15,
                       [("WMob%d" % s,)] + YB, [PSK[bB]])
                tt(gbf, ps[bB][:, :], gbf, ALU.mult, [PSK[bB], ("gbf",)], [("gbf",)])
                for kc in range(16):
                    mm(ps[bA][:, :], WM[("oa", s)][:, kc, c * 128:(c + 1) * 128], yaT[:, kc, tsl], kc == 0, kc == 15,
                       [("WMoa%d" % s,)] + [("yaT", ch, th) for ch in range(16)], [PSK[bA]])
                tt(gaf, ps[bA][:, :], gaf, ALU.mult, [PSK[bA], ("gaf",)], [("gaf",)])
                tt(mixT[:, n, tsl], gaf, gbf, ALU.add, [("gaf",), ("gbf",)], [("mixT", n, th)])

    dbg("mixT", mixT, [("mixT", n, th) for n in range(16) for th in range(2)], [16, 1024], BF16)
    hT = VB("h", 96 * KB, [8, 2048], F32)
    WO = [VB("WO%d" % s, 160 * KB + s * 16 * KB, [16, 512], BF16) for s in range(2)]
    xs = [VB("xs%d" % s, 192 * KB + s * 2 * KB, [512], F32) for s in range(2)]
    g2 = VB("g2", 64 * KB, [2048], F32)
    xn2 = [VB("xn2%d" % s, 72 * KB + s * 4 * KB, [2048], BF16) for s in range(2)]
    sqj3 = VB("sqj3", 80 * KB, [2048], F32)
    hnT = VB("hnT", 32 * KB, [16, 1024], BF16)
    ld(g2, g2bc_d, ("g2",))
    MX = [("mixT", n, th) for n in range(16) for th in range(2)]
    ob_ = Rot([2, 3, 4, 5])
    xr = Rot([0, 1])
    for np_ in range(4):
        s = np_ % 2
        ldw(WO[s], wout[:, :, np_ * 512:(np_ + 1) * 512], ("WO%d" % s,))
        for T in range(8):
            b = ob_.next()
            xsl = xr.next()
            ld(xs[xsl], x_own[T * 128:(T + 1) * 128, np_ * 512:(np_ + 1) * 512], ("xs%d" % xsl,))
            for kc in range(16):
                mm(ps[b][:, :], mixT[:, kc, T * 128:(T + 1) * 128], WO[s][:, kc, :], kc == 0, kc == 15,
                   [("WO%d" % s,)] + MX, [PSK[b]])
            tt(hT[:, T, np_ * 512:(np_ + 1) * 512], ps[b][:, :], xs[xsl], ALU.add, [PSK[b], ("xs%d" % xsl,)], [("h", T, np_)])
            if np_ == 3:
                act(sqj3, hT[:, T, :], AF.Square, [("h", T, q) for q in range(4)] + [("sm", "acc")],
                    [("sqj3",), ("sm", "ss2", T)], accum_out=ss2[:, T:T + 1])

    def norm_from_h(ss_ap, rs_ap, tag, jk, jkey):
        rkey = ("sm", tag + "r")
        rstd_batch(ss_ap[:, 0:8], rs_ap[:, 0:8], 1.0 / 2048, [("sm", tag, T) for T in range(8)], rkey)
        return rkey

    dbg("h1", hT, [("h", T, q) for T in range(8) for q in range(4)], [8, 2048], F32)
    rkey2 = norm_from_h(ss2, rs2, "ss2", sqj3, ("sqj3",))
    def hn_stt(T):
        HK = [("h", T, q) for q in range(4)]
        s = T % 2
        stt(xn2[s], hT[:, T, :], rs2[:, T:T + 1], g2, ALU.mult, ALU.mult, HK + [rkey2, ("g2",)], [("xn2%d" % s,)])

    hn_stt(0)
    for T in range(8):
        s = T % 2
        if T + 1 < 8:
            hn_stt(T + 1)
        for kc in range(16):
            tr(psb[kc // 8][:, (kc % 8) * 128:(kc % 8 + 1) * 128], xn2[s][:, kc * 128:(kc + 1) * 128],
               [("xn2%d" % s,)], [PSK[kc // 8]])
        for half in range(2):
            cp(evac_rot.next(), hnT[:, half * 8:(half + 1) * 8, T * 128:(T + 1) * 128],
               psb[half][:, 0:1024].rearrange("p (a b) -> p a b", a=8), [PSK[half]], [("hnT", T)])

    actT = VB("actT", 0, [12, 1024], BF16)
    WG = [VB("WG%d" % s, 160 * KB + s * 8 * KB, [16, 256], BF16) for s in range(2)]
    WU = [VB("WU%d" % s, 176 * KB + s * 8 * KB, [16, 256], BF16) for s in range(2)]
    sgf = [VB("sgf%d" % s, 192 * KB + s * 2 * KB, [512], F32) for s in range(2)]
    WD = [VB("WD%d" % s, 64 * KB + s * 12 * KB, [12, 512], BF16) for s in range(2)]
    gf = VB("gf", 88 * KB, [2048], F32)
    ld(gf, gfbc_d, ("gf",))
    HN = [("hnT", T) for T in range(8)]
    fb = Rot([0, 1, 2, 3])
    db = Rot([4, 5, 6, 7])
    sgr = Rot([0, 1])
    wdr = Rot([0, 1])
    pidx = 0
    for grp, npan in enumerate((6, 6, 6, 4)):
        nf = npan * 2
        for pl in range(npan):
            s = pidx % 2
            c0 = pidx * 256
            ldw(WG[s], wfg[:, :, c0:c0 + 256], ("WG%d" % s,))
            ldw(WU[s], wfu[:, :, c0:c0 + 256], ("WU%d" % s,))
            for c in range(2):
                fl = pl * 2 + c
                for th in range(2):
                    bG, bU = fb.next(), fb.next()
                    tsl = slice(th * 512, (th + 1) * 512)
                    for kc in range(16):
                        mm(ps[bG][:, :], WG[s][:, kc, c * 128:(c + 1) * 128], hnT[:, kc, tsl], kc == 0, kc == 15,
                           [("WG%d" % s,)] + HN[4 * th:4 * th + 4], [PSK[bG]])
                    q = sgr.next()
                    act(sgf[q], ps[bG][:, :], AF.Silu, [PSK[bG]], [("sgf%d" % q,)])
                    for kc in range(16):
                        mm(ps[bU][:, :], WU[s][:, kc, c * 128:(c + 1) * 128], hnT[:, kc, tsl], kc == 0, kc == 15,
                           [("WU%d" % s,)] + HN[4 * th:4 * th + 4], [PSK[bU]])
                    tt(actT[:, fl, tsl], ps[bU][:, :], sgf[q], ALU.mult, [PSK[bU], ("sgf%d" % q,)], [("actT", fl, th)])
            pidx += 1
        f0 = (pidx - npan) * 2
        AK = [("actT", fl, th) for fl in range(nf) for th in range(2)]
        for np_ in range(4):
            s = wdr.next()
            ldw(WD[s][:, 0:nf, :], wfd[:, f0:f0 + nf, np_ * 512:(np_ + 1) * 512], ("WD%d" % s,))
            for T in range(8):
                b = db.next()
                for fl in range(nf):
                    mm(ps[b][:, :], actT[:, fl, T * 128:(T + 1) * 128], WD[s][:, fl, :], fl == 0, fl == nf - 1,
                       [("WD%d" % s,)] + AK, [PSK[b]])
                hsl = hT[:, T, np_ * 512:(np_ + 1) * 512]
                tt(hsl, ps[b][:, :], hsl, ALU.add, [PSK[b], ("h", T, np_)], [("h", T, np_)])
                if grp == 3 and np_ == 3:
                    if T == 0:
                        sqjF = VB("sqjF", 160 * KB, [2048], F32)
                    act(sqjF, hT[:, T, :], AF.Square, [("h", T, q) for q in range(4)] + [("sm", "acc")],
                        [("sqjF",), ("sm", "ssF", T)], accum_out=ssF[:, T:T + 1])

    dbg("hnT", hnT, HN, [16, 1024], BF16)
    dbg("h2", hT, [("h", T, q) for T in range(8) for q in range(4)], [8, 2048], F32)
    rkeyF = norm_from_h(ssF, rsF, "ssF", None, None)
    for T in range(8):
        HK = [("h", T, q) for q in range(4)]
        stt(hT[:, T, :], hT[:, T, :], rsF[:, T:T + 1], gf, ALU.mult, ALU.mult, HK + [rkeyF, ("gf",)], HK)
        P.dma("sp", lambda e, T=T: e.dma_start(out=out_d[T * 128:(T + 1) * 128, :], in_=hT[:, T, :]), HK, [("out", T)])

    P.emit(st)
    st.close()
    P.dbg_names = dbg_names
    return nc, P


_CACHE = {}


def kernel(x, norm1_g, w_in, a_ln_g, a_ln_b, a_w_s, a_b_s, kv_norm_g, w_uk, w_uv, w_oa, w_ob, w_out,
           norm2_g, w_ff_gate, w_ff_up, w_ff_down, final_g):
    f = np.float32
    x = np.asarray(x, f)
    if "nc" not in _CACHE:
        _CACHE["nc"] = build_program()[0]
    nc = _CACHE["nc"]

    def bc(v, n):
        return np.ascontiguousarray(np.broadcast_to(np.asarray(v, f).reshape(1, n), (128, n)))

    shared = {
        "ident": np.eye(128, dtype=f),
        "pow2": bc(np.array([2.0 ** -(k + 1) for k in range(32)], f), 32),
        "g1bc": bc(norm1_g[0], 2048),
        "kvgbc": bc(kv_norm_g[0], 512),
        "lngbc": bc(a_ln_g[0], 2048),
        "lnbbc": bc(a_ln_b[0], 2048),
        "bsbc": bc(np.repeat(np.asarray(a_b_s[0], f), 2, axis=0).reshape(-1), 2048),
        "wsT": np.ascontiguousarray(np.transpose(np.asarray(a_w_s[0], f), (2, 0, 1)).reshape(128, 1024)),
        "g2bc": bc(norm2_g[0], 2048),
        "gfbc": bc(final_g, 2048),
        "w_in": np.ascontiguousarray(np.asarray(w_in[0], f)),
        "w_uk": np.ascontiguousarray(np.asarray(w_uk[0], f).reshape(512, 2048)),
        "w_uv": np.ascontiguousarray(np.asarray(w_uv[0], f).reshape(512, 2048)),
        "w_oa": np.ascontiguousarray(np.asarray(w_oa[0], f)),
        "w_ob": np.ascontiguousarray(np.asarray(w_ob[0], f)),
        "w_out": np.ascontiguousarray(np.asarray(w_out[0], f)),
        "w_ff_gate": np.ascontiguousarray(np.asarray(w_ff_gate[0], f)),
        "w_ff_up": np.ascontiguousarray(np.asarray(w_ff_up[0], f)),
        "w_ff_down": np.ascontiguousarray(np.asarray(w_ff_down[0], f)),
    }
    in_maps = []
    for c in range(8):
        b, j = c // 4, c % 4
        xb = x[b]
        own = np.ascontiguousarray(xb.reshape(8, 4, 128, 2048)[:, j].reshape(1024, 2048))
        sp = np.arange(512)[None, :]
        r = np.arange(128)[:, None]
        adm = (sp < 128 * j + 64) | ((sp < 128 * j + 128) & (r >= 64))
        admb = np.where(adm, 0.0, -1e30).astype(f)
        m = dict(shared)
        m["x_all"] = np.ascontiguousarray(xb)
        m["x_own"] = own
        m["admb"] = admb
        in_maps.append(m)
    if _CACHE.get("return_maps"):
        return in_maps
    res = run_bass_kernel_spmd(nc, in_maps, core_ids=list(range(8)))
    out = np.empty((2, 4096, 2048), f)
    for c in range(8):
        b, j = c // 4, c % 4
        out[b].reshape(8, 4, 128, 2048)[:, j] = np.asarray(res.results[c]["out"], f).reshape(8, 128, 2048)
    return out
```
